# Optimizing a Trainium2 kernel written in Bass

```python
import jax, jax.numpy as jnp
from jax import lax
import numpy as np

D_MODEL = 1024
BATCH = 1
SEQ = 16384
DEPTH = 1

W_A = 3 * D_MODEL // 2
CONV_K = 3
CHUNK = 128
DH_B = 128
H_B = D_MODEL // DH_B
W_B = H_B * DH_B
D_FF = 4 * D_MODEL
N_PROJ = 3 * W_A + 2 * W_B + 2 * D_MODEL
LN_EPS = 1e-5
ALPHA = (2.0 * DEPTH) ** 0.25
BETA = (8.0 * DEPTH) ** -0.25

OFF_BA = 0
OFF_CA = OFF_BA + W_A
OFF_HA = OFF_CA + W_A
OFF_UB = OFF_HA + W_A
OFF_VB = OFF_UB + W_B
OFF_GA = OFF_VB + W_B
OFF_GB = OFF_GA + D_MODEL

kernel_name = "hybrid_shortconv_gmlp_deepnorm_encoder"


def _layernorm(x, g, b):
    xf = x.astype(jnp.float32)
    mu = jnp.mean(xf, axis=-1, keepdims=True)
    var = jnp.mean(jnp.square(xf - mu), axis=-1, keepdims=True)
    y = (xf - mu) * lax.rsqrt(var + LN_EPS) * g.astype(jnp.float32) + b.astype(jnp.float32)
    return y.astype(x.dtype)


def _centred_depthwise_conv(h, w):
    hp = jnp.pad(h, ((0, 0), (1, 1), (0, 0)))
    return hp[:, :-2, :] * w[0] + hp[:, 1:-1, :] * w[1] + hp[:, 2:, :] * w[2]


def _spatial_gate(u, v, g, b, w_s, b_s):
    bsz = v.shape[0]
    s = v.shape[1]
    n_chunks = s // CHUNK
    v = _layernorm(v, g, b)
    vc = v.reshape(bsz, n_chunks, CHUNK, H_B, DH_B)
    mixed = jnp.einsum('hij,bcjhd->bcihd', w_s, vc)
    mixed = mixed + jnp.transpose(b_s)[None, None, :, :, None]
    return u * mixed.reshape(bsz, s, W_B)


def setup_inputs(seed: int = 0) -> dict:
    key = jax.random.key(seed)
    ks = jax.random.split(key, 20)
    L = DEPTH

    def nrm(k, shape, scale):
        return jax.random.normal(k, shape, jnp.float32) * scale

    return {
        "x": nrm(ks[0], (BATCH, SEQ, D_MODEL), 1.0),
        "w_in": nrm(ks[1], (L, D_MODEL, N_PROJ), D_MODEL ** -0.5),
        "b_gate": nrm(ks[2], (L, 2 * D_MODEL), 0.02),
        "conv_w": nrm(ks[3], (L, CONV_K, W_A), CONV_K ** -0.5),
        "v_norm_g": 1.0 + nrm(ks[4], (L, W_B), 0.02),
        "v_norm_b": nrm(ks[5], (L, W_B), 0.02),
        "w_s": nrm(ks[6], (L, H_B, CHUNK, CHUNK), CHUNK ** -0.5),
        "b_s": 1.0 + nrm(ks[7], (L, H_B, CHUNK), 0.02),
        "w_pa": nrm(ks[8], (L, W_A, D_MODEL), W_A ** -0.5),
        "w_pb": nrm(ks[9], (L, W_B, D_MODEL), W_B ** -0.5),
        "w_o": nrm(ks[10], (L, D_MODEL, D_MODEL), BETA * D_MODEL ** -0.5),
        "ln1_g": 1.0 + nrm(ks[11], (L, D_MODEL), 0.02),
        "ln1_b": nrm(ks[12], (L, D_MODEL), 0.02),
        "w_ff1": nrm(ks[13], (L, D_MODEL, D_FF), BETA * D_MODEL ** -0.5),
        "w_ff2": nrm(ks[14], (L, D_FF, D_MODEL), BETA * D_FF ** -0.5),
        "ln2_g": 1.0 + nrm(ks[15], (L, D_MODEL), 0.02),
        "ln2_b": nrm(ks[16], (L, D_MODEL), 0.02),
    }


def reference(x, w_in, b_gate, conv_w, v_norm_g, v_norm_b, w_s, b_s, w_pa, w_pb, w_o,
              ln1_g, ln1_b, w_ff1, w_ff2, ln2_g, ln2_b):
    for l in range(DEPTH):
        p = jnp.einsum('bsd,dn->bsn', x, w_in[l])
        b_a = p[:, :, OFF_BA:OFF_CA]
        c_a = p[:, :, OFF_CA:OFF_HA]
        h_a = p[:, :, OFF_HA:OFF_UB]
        u_b = p[:, :, OFF_UB:OFF_VB]
        v_b = p[:, :, OFF_VB:OFF_GA]
        gates = jax.nn.sigmoid(p[:, :, OFF_GA:N_PROJ] + b_gate[l])
        g_a = gates[:, :, :D_MODEL]
        g_b = gates[:, :, D_MODEL:]
        a = b_a * _centred_depthwise_conv(c_a * h_a, conv_w[l])
        bb = _spatial_gate(jax.nn.gelu(u_b), jax.nn.gelu(v_b), v_norm_g[l], v_norm_b[l],
                           w_s[l], b_s[l])
        y_a = jnp.einsum('bsc,cd->bsd', a, w_pa[l])
        y_b = jnp.einsum('bsc,cd->bsd', bb, w_pb[l])
        mix = jnp.einsum('bsd,de->bse', g_a * y_a + g_b * y_b, w_o[l])
        x = _layernorm(ALPHA * x + mix, ln1_g[l], ln1_b[l])
        hid = jnp.square(jax.nn.relu(jnp.einsum('bsd,df->bsf', x, w_ff1[l])))
        ffn = jnp.einsum('bsf,fd->bsd', hid, w_ff2[l])
        x = _layernorm(ALPHA * x + ffn, ln2_g[l], ln2_b[l])
    return x
```

```python
import numpy as np
import concourse.bass as bass
import concourse.mybir as mybir
from concourse.bass_utils import run_bass_kernel_spmd

F32 = mybir.dt.float32
BF16 = mybir.dt.bfloat16
I32 = mybir.dt.int32
AF = mybir.ActivationFunctionType
ALU = mybir.AluOpType

NCORES = 8
D = 1024
SEQ = 16384
TOK = SEQ // NCORES
T = 1024
ST = TOK // T
W_A = 1536
D_FF = 4096
N_PROJ = 8704
OFF_BA, OFF_CA, OFF_HA, OFF_UB, OFF_VB, OFF_GA, OFF_GB = 0, 1536, 3072, 4608, 5632, 6656, 7680
LN_EPS = 1e-5
ALPHA = float(2.0 ** 0.25)
NS = 8
NT4 = 5
NT2 = 4
N_WARM = 36
NORM_LAG = 5
CV_CW0, CV_CW1, CV_CW2, CV_BGA, CV_BGB, CV_VG, CV_VB, CV_L1G, CV_L1B = 0, 12, 24, 36, 44, 52, 60, 68, 76
NCV = 84


class Buf:
    def __init__(self, accum=False):
        self.w = {}
        self.r = {}
        self.accum = accum


def _merge(d, tok):
    k = id(tok[0])
    if k not in d or d[k][1] < tok[1]:
        d[k] = tok


def deps(reads, writes):
    toks = []
    for b in reads:
        toks += list(b.w.values())
    for b in writes:
        if b.accum:
            continue
        toks += list(b.w.values())
        toks += list(b.r.values())
    return toks


def commit(tok, reads, writes):
    for b in reads:
        _merge(b.r, tok)
    for b in writes:
        if b.accum:
            _merge(b.w, tok)
        else:
            b.w = {id(tok[0]): tok}
            b.r = {}


class Q:
    def __init__(self, name, sem):
        self.name = name
        self.sem = sem
        self.cnt = 0
        self.seen = {}
        self.items = []

    def wait(self, toks):
        need = {}
        for t in toks:
            sem, v = t
            if self.name == "pe" and sem is self.sem:
                continue
            k = id(sem)
            if self.seen.get(k, 0) >= v:
                continue
            if k not in need or need[k][1] < v:
                need[k] = (sem, v)
        for sem, v in need.values():
            self.seen[id(sem)] = v
            self.items.append(lambda e, sem=sem, v=v: e.wait_ge(sem, v))

    def run(self, fn):
        self.cnt += 1
        sem = self.sem
        self.items.append(lambda e, fn=fn, sem=sem: fn(e).then_inc(sem, 1))
        return (sem, self.cnt)

    def run_nomark(self, fn):
        self.items.append(lambda e, fn=fn: fn(e))

    def dma(self, out, in_, semst):
        semst[1] += 16
        sem = semst[0]
        self.items.append(lambda e, out=out, in_=in_, sem=sem: e.dma_start(out=out, in_=in_).then_inc(sem, 16))
        return (sem, semst[1])


def build_program():
    nc = bass.Bass("TRN2", target_bir_lowering=False)
    x_d = nc.dram_tensor("x", [TOK, D], F32, kind="ExternalInput").ap()
    xh_d = nc.dram_tensor("xhT", [128, ST * 16], F32, kind="ExternalInput").ap()
    xTd = nc.dram_tensor("xT", [D, TOK], F32, kind="ExternalInput").ap()
    win_d = nc.dram_tensor("w_in", [D, N_PROJ], F32, kind="ExternalInput").ap()
    wpa_d = nc.dram_tensor("w_pa", [W_A, D], F32, kind="ExternalInput").ap()
    wpb_d = nc.dram_tensor("w_pb", [D, D], F32, kind="ExternalInput").ap()
    wo_d = nc.dram_tensor("w_o", [D, D], F32, kind="ExternalInput").ap()
    wf1_d = nc.dram_tensor("w_ff1", [D, D_FF], F32, kind="ExternalInput").ap()
    wf2_d = nc.dram_tensor("w_ff2", [D_FF, D], F32, kind="ExternalInput").ap()
    cv_d = nc.dram_tensor("colvecs", [128, NCV], F32, kind="ExternalInput").ap()
    wsT_d = nc.dram_tensor("wsT", [128, 1024], F32, kind="ExternalInput").ap()
    bs_d = nc.dram_tensor("bsrow", [1, 1024], F32, kind="ExternalInput").ap()
    g2_d = nc.dram_tensor("ln2g", [1, 1024], F32, kind="ExternalInput").ap()
    b2_d = nc.dram_tensor("ln2b", [1, 1024], F32, kind="ExternalInput").ap()
    y_d = nc.dram_tensor("y", [TOK, D], F32, kind="ExternalOutput").ap()

    xTr = xTd.rearrange("(k p) n -> p k n", p=128)
    winr = win_d.rearrange("(k p) n -> p k n", p=128)
    wpar = wpa_d.rearrange("(k p) n -> p k n", p=128)
    wpbr = wpb_d.rearrange("(k p) n -> p k n", p=128)
    wor = wo_d.rearrange("(k p) n -> p k n", p=128)
    wf1r = wf1_d.rearrange("(k p) n -> p k n", p=128)
    wf2r = wf2_d.rearrange("(k p) n -> p k n", p=128)

    from contextlib import ExitStack
    es = ExitStack()
    with es:
        def sb(name, shape, dt):
            return es.enter_context(nc.sbuf_tensor(name, shape, dt))

        def semaphore(name):
            return es.enter_context(nc.semaphore(name))

        regP = sb("regP", [128, 32768], BF16)
        regQ = sb("regQ", [128, 8192], F32)
        regR = sb("regR", [128, 8, T], BF16)
        ring = sb("ring", [128, NS, 2048], BF16)
        bigW = sb("bigW", [128, 8, 1024], BF16)
        t4t = [sb(f"t4_{i}", [128, 1032], F32) for i in range(NT4)]
        t2t = [sb(f"t2_{i}", [128, 512], F32) for i in range(NT2)]
        Cc = sb("Cc", [128, 8, 128], F32)
        g2rep = sb("g2rep", [128, 1024], F32)
        b2rep = sb("b2rep", [128, 1024], F32)
        wsTb = sb("wsTb", [128, 8, 128], BF16)
        ident = sb("ident", [128, 128], F32)
        ones = sb("ones", [128, 128], F32)
        colv = sb("colv", [128, NCV], F32)
        chalf = sb("chalf", [128, 2], F32)
        xhT = sb("xhTs", [128, ST * 16], BF16)
        NSTAT = 6
        statt = [sb(f"stat{i}", [128, 32], F32) for i in range(NSTAT)]
        ps = es.enter_context(nc.psum_tensor("ps", [128, 8, 512], F32))

        a_sb = regP[:, 0:12288].rearrange("p (k n) -> p k n", k=12)
        bb_sb = regP[:, 12288:20480].rearrange("p (k n) -> p k n", k=8)
        vn_sb = regP[:, 20480:28672].rearrange("p (k n) -> p k n", k=8)
        z_sb = regP[:, 20480:28672].rearrange("p (k n) -> p k n", k=8)
        hid = regP[:, 0:32768].rearrange("p (k n) -> p k n", k=32)
        xT = regR
        x1Tb = regR
        x1Tf = regQ[:, 0:8192].rearrange("p (k n) -> p k n", k=8)

        pe = Q("pe", semaphore("s_pe"))
        act = Q("act", semaphore("s_act"))
        dve = Q("dve", semaphore("s_dve"))
        pool = Q("pool", semaphore("s_pool"))
        sp = Q("sp", semaphore("s_sp"))

        def OP(q, fn, reads=(), writes=()):
            for b in reads:
                b.open = False
            q.wait(deps(reads, writes))
            tok = q.run(fn)
            commit(tok, reads, writes)
            return tok

        deferred = []

        def tick():
            ready = []
            for d in deferred:
                d[0] -= 1
                if d[0] <= 0:
                    ready.append(d)
            for d in ready:
                deferred.remove(d)
            for d in ready:
                d[1]()

        def defer(n, fn):
            deferred.append([n, fn])

        def flush_deferred():
            while deferred:
                d = deferred.pop(0)
                d[1]()

        def PEG(fns, reads, writes, do_tick=True):
            pe.wait(deps(reads, writes))
            for f in fns[:-1]:
                pe.run_nomark(f)
            tok = pe.run(fns[-1])
            commit(tok, reads, writes)
            for b in writes:
                b.open = True
            if do_tick:
                tick()
            return tok

        def DMA(q, out, in_, semst, reads=(), writes=()):
            q.wait(deps(reads, writes))
            tok = q.dma(out, in_, semst)
            commit(tok, reads, writes)
            return tok

        def mm(out, l, r, st, sp_):
            return lambda e: e.matmul(out, lhsT=l, rhs=r, start=st, stop=sp_)

        def tr(out, in_):
            return lambda e: e.transpose(out=out, in_=in_, identity=ident[:])

        bankB = [Buf() for _ in range(8)]
        cur = [0]

        nbanks = [8]
        slowc = [0]

        def nb():
            i = cur[0] % nbanks[0]
            cur[0] = (i + 1) % nbanks[0]
            assert not getattr(bankB[i], "open", False), "PSUM bank re-allocated before its reader was emitted"
            return i

        def nb_slow():
            i = 6 + slowc[0] % 2
            slowc[0] += 1
            assert not getattr(bankB[i], "open", False), "PSUM bank re-allocated before its reader was emitted"
            return i

        def nb2():
            c = cur[0] % nbanks[0]
            if c % 2:
                c = (c + 1) % nbanks[0]
            i = c
            cur[0] = (i + 2) % nbanks[0]
            assert not getattr(bankB[i], "open", False) and not getattr(bankB[i + 1], "open", False), "PSUM bank re-allocated before its reader was emitted"
            return i

        class Tmp:
            def __init__(self, t, name):
                self.t = t
                self.B = Buf()
                self.ld = [semaphore("ld_" + name), 0]
                self.st = [semaphore("st_" + name), 0]

        T4 = [Tmp(t, f"t4_{i}") for i, t in enumerate(t4t)]
        NX = [Tmp(regP[:, 28672 + i * 2048:28672 + (i + 1) * 2048].bitcast(F32), f"nx_{i}") for i in range(2)]
        nxc = [0]
        T2 = [Tmp(t, f"t2_{i}") for i, t in enumerate(t2t)]
        c4 = [0]
        c2 = [0]
        cs = [0]

        def t4():
            r = T4[c4[0] % NT4]
            c4[0] += 1
            return r

        def t2():
            r = T2[c2[0] % NT2]
            c2[0] += 1
            return r

        class GvTmp:
            def __init__(self, i):
                self.t = regQ[:, i * 1024:(i + 1) * 1024]
                self.B = Buf()
                self.i = i

        GV = [GvTmp(i) for i in range(8)]
        cg = [0]

        def gvbuf():
            r = GV[cg[0] % 8]
            cg[0] += 1
            return r

        statB = [Buf() for _ in range(NSTAT)]

        def nstat():
            i = cs[0] % NSTAT
            cs[0] += 1
            return statt[i], statB[i]

        xTB = [[Buf(), Buf()] for _ in range(8)]
        aB = [[Buf(), Buf()] for _ in range(12)]
        vnB = [Buf() for _ in range(8)]
        bbB = [[Buf(), Buf()] for _ in range(8)]
        zB = [[Buf(), Buf()] for _ in range(8)]
        x1fB = [[Buf(), Buf()] for _ in range(8)]
        x1bB = [Buf() for _ in range(8)]
        r2B = [[Buf() for _ in range(4)] for _ in range(8)]
        hidB = [[Buf(), Buf()] for _ in range(32)]
        memB = Buf(accum=True)
        colvB = Buf(accum=True)
        xhTB = Buf(accum=True)
        wsTbB = Buf(accum=True)
        CcB = Buf(accum=True)
        g2B = Buf(accum=True)
        b2B = Buf(accum=True)
        bigWB = Buf()
        bigW_sem = [semaphore("bigw"), 0]
        sem_xh = [semaphore("ld_xh"), 0]
        sem_ws = [semaphore("ld_ws"), 0]
        sem_cv = [semaphore("ld_cv"), 0]
        sem_g2 = [semaphore("ld_g2"), 0]
        sem_b2 = [semaphore("ld_b2"), 0]
        slotB = [Buf() for _ in range(NS)]
        slot_sem = [[semaphore(f"slot{i}"), 0] for i in range(NS)]

        blocks = []

        def add_block(src, nk, ncol):
            blocks.append((src, nk, ncol))
            return len(blocks) - 1

        plan = []
        for st in range(ST):
            p = {}
            p["A"] = [None] * 6
            p["U"] = [None] * 4

            def addA(g):
                c = add_block(winr[:, :, OFF_CA + g * 256:OFF_CA + (g + 1) * 256], 8, 256)
                h = add_block(winr[:, :, OFF_HA + g * 256:OFF_HA + (g + 1) * 256], 8, 256)
                b = add_block(winr[:, :, OFF_BA + g * 256:OFF_BA + (g + 1) * 256], 8, 256)
                p["A"][g] = (c, h, b)

            def addU(g):
                p["U"][g] = add_block(winr[:, :, OFF_UB + g * 256:OFF_UB + (g + 1) * 256], 8, 256)

            for g in range(5):
                addA(g)
            addU(0)
            addU(1)
            addA(5)
            addU(2)
            addU(3)
            p["YZ"] = []
            for g in range(4):
                pa0 = add_block(wpar[:, 0:8, g * 256:(g + 1) * 256], 8, 256)
                pa1 = add_block(wpar[:, 8:12, g * 256:(g + 1) * 256], 4, 256)
                pb = add_block(wpbr[:, :, g * 256:(g + 1) * 256], 8, 256)
                ga = add_block(winr[:, :, OFF_GA + g * 256:OFF_GA + (g + 1) * 256], 8, 256)
                gb = add_block(winr[:, :, OFF_GB + g * 256:OFF_GB + (g + 1) * 256], 8, 256)
                p["YZ"].append((pa0, pa1, pb, ga, gb))
            p["F1"] = [add_block(wf1r[:, :, g * 256:(g + 1) * 256], 8, 256) for g in range(16)]
            p["F2"] = []
            for g in range(4):
                p["F2"].append([add_block(wf2r[:, q * 8:(q + 1) * 8, g * 256:(g + 1) * 256], 8, 256) for q in range(4)])
            if st == ST - 1:
                for key in ("F2b",):
                    p[key] = [None]
                    for g in range(1, 4):
                        p[key].append([add_block(wf2r[:, q * 8:(q + 1) * 8, g * 256:(g + 1) * 256], 8, 256) for q in range(4)])
            plan.append(p)
        issued = set()
        released = set()

        def issue(i):
            assert i not in issued
            issued.add(i)
            src, nk, ncol = blocks[i]
            s = i % NS
            DMA(pool, ring[:, s, 0:nk * ncol].rearrange("p (k n) -> p k n", k=nk), src, slot_sem[s], writes=[slotB[s]])

        def blk(i):
            assert i in issued and (i + NS) not in issued, i
            src, nk, ncol = blocks[i]
            s = i % NS
            return slotB[s], ring[:, s, 0:nk * ncol].rearrange("p (k n) -> p k n", k=nk)

        def release(i):
            assert i not in released
            released.add(i)
            j = i + NS
            if j < len(blocks):
                issue(j)

        def ln_stats(src_aps, srcbufs):
            stt_, sB = nstat()
            for i, ap in enumerate(src_aps):
                OP(dve, lambda e, ap=ap, i=i: e.bn_stats(out=stt_[:, i * 6:(i + 1) * 6], in_=ap), reads=srcbufs, writes=[sB])
            n = len(src_aps)
            OP(dve, lambda e: e.bn_aggr(out=stt_[:, 12:14], in_=stt_[:, 0:6 * n]), reads=[sB], writes=[sB])
            OP(pool, lambda e: e.tensor_scalar(out=stt_[:, 14:15], in0=stt_[:, 13:14], scalar1=LN_EPS, scalar2=1.0, op0=ALU.add, op1=ALU.mult), reads=[sB], writes=[sB])
            OP(pool, lambda e: e.tensor_tensor(out=stt_[:, 15:16], in0=stt_[:, 14:15], in1=chalf[:, 0:1], op=ALU.pow), reads=[sB, memB], writes=[sB])
            OP(pool, lambda e: e.tensor_scalar(out=stt_[:, 21:22], in0=stt_[:, 12:13], scalar1=stt_[:, 15:16], scalar2=-1.0, op0=ALU.mult, op1=ALU.mult), reads=[sB], writes=[sB])
            return stt_, sB, 15

        OP(pool, lambda e: e.memset(ident[:], 0.0), writes=[memB])
        OP(pool, lambda e: e.affine_select(out=ident[:], in_=ident[:], compare_op=ALU.not_equal, fill=1.0, base=0, pattern=[[-1, 128]], channel_multiplier=1), reads=[memB], writes=[memB])
        OP(pool, lambda e: e.memset(ones[:], 1.0), writes=[memB])
        OP(pool, lambda e: e.memset(chalf[:], -0.5), writes=[memB])
        xT_sem = [[semaphore("xT0"), 0], [semaphore("xT1"), 0]]

        def load_xT(st, half):
            tts = range(half * 4, half * 4 + 4)
            wr = [xTB[tt][k] for tt in tts for k in range(2)] + [x1bB[tt] for tt in tts]
            c0 = st * T + half * 512
            DMA(pool, xT[:, :, half * 512:(half + 1) * 512], xTr[:, :, c0:c0 + 512], xT_sem[half], writes=wr)

        def setup_Cc():
            wsf = t4()
            bsr = t4()
            DMA(sp, wsf.t[:, 0:1024], wsT_d, wsf.ld, writes=[wsf.B])
            DMA(sp, bsr.t[:, 0:1024], bs_d.broadcast_to([128, 1024]), bsr.ld, writes=[bsr.B])
            b0 = nb2()
            PEG([mm(ps[:, b0 + h // 4, (h % 4) * 128:(h % 4 + 1) * 128], ones[:], wsf.t[:, h * 128:(h + 1) * 128], True, True) for h in range(8)],
                reads=[memB, wsf.B], writes=[bankB[b0], bankB[b0 + 1]], do_tick=False)
            PEG([mm(ps[:, 7, 0:128], ones[:], ones[:], True, True) for _ in range(N_WARM)],
                reads=[memB], writes=[bankB[7]], do_tick=False)
            bankB[7].open = False
            load_xT(0, 0)
            issue(0)
            DMA(sp, colv[:], cv_d, sem_cv, writes=[colvB])
            pool.wait(list(slotB[0].w.values()) + list(xTB[0][0].w.values()))
            issue(1)
            load_xT(0, 1)
            DMA(pool, xhT[:], xh_d, sem_xh, writes=[xhTB])
            pool.wait(list(slotB[1].w.values()) + list(xTB[4][0].w.values()))
            issue(2)
            DMA(pool, wsTb[:], wsT_d.rearrange("p (h i) -> p h i", h=8), sem_ws, writes=[wsTbB])
            for i in range(3, NS):
                issue(i)
            DMA(pool, bigW[:], winr[:, :, OFF_VB:OFF_VB + 1024], bigW_sem, writes=[bigWB])
            DMA(sp, g2rep[:], g2_d.broadcast_to([128, 1024]), sem_g2, writes=[g2B])
            DMA(sp, b2rep[:], b2_d.broadcast_to([128, 1024]), sem_b2, writes=[b2B])
            for h in range(8):
                OP(dve, lambda e, h=h: e.scalar_tensor_tensor(out=Cc[:, h, :], in0=ps[:, b0 + h // 4, (h % 4) * 128:(h % 4 + 1) * 128],
                                                             scalar=colv[:, CV_VB + h:CV_VB + h + 1], in1=bsr.t[:, h * 128:(h + 1) * 128],
                                                             op0=ALU.mult, op1=ALU.add),
                   reads=[bankB[b0 + h // 4], bsr.B, colvB], writes=[CcB])

        def xT_reads(half):
            r = []
            for tt in range(half * 4, half * 4 + 4):
                r += xTB[tt]
            return r

        def side_work(st, ca, slot):
            if ca < 4:
                if slot in (0, 2) and pending_tails:
                    pending_tails.pop(0)()
            elif ca < 8:
                if slot in (0, 2):
                    v_unit(st, 2 * (ca - 4) + slot // 2)
                if ca == 7 and slot == 3:
                    DMA(pool, bigW[:], wor[:, :, :], bigW_sem, writes=[bigWB])
            else:
                if ca == 8 and slot == 0:
                    flush_deferred()
                su_unit(st, 2 * (ca - 8) + slot // 2, slot % 2)

        def stage_A(st):
            nbanks[0] = 6
            for ca in range(12):
                g, j = ca // 2, ca % 2
                ic, ih, ib = plan[st]["A"][g]
                WCB, WC = blk(ic)
                WHB, WH = blk(ih)
                WBB, WB = blk(ib)
                js = slice(j * 128, (j + 1) * 128)
                hc = t4()
                cv = t4()
                xo = st * 16
                bx = None
                for half in range(2):
                    hs = slice(half * 512, (half + 1) * 512)
                    if half == 1:
                        bx = nb()
                    bc = nb()
                    fns = []
                    for kc in range(8):
                        fns.append(mm(ps[:, bc, :], WC[:, kc, js], xT[:, kc, hs], kc == 0, kc == 7))
                        if half == 1:
                            fns.append(mm(ps[:, bx, 0:2], WC[:, kc, js], xhT[:, xo + kc * 2:xo + kc * 2 + 2], kc == 0, kc == 7))
                    PEG(fns, reads=[WCB] + ([xhTB] if half == 1 else []) + xT_reads(half), writes=[bankB[bc]] + ([bankB[bx]] if half == 1 else []))
                    bh = nb()
                    fns = []
                    for kc in range(8):
                        fns.append(mm(ps[:, bh, :], WH[:, kc, js], xT[:, kc, hs], kc == 0, kc == 7))
                        if half == 1:
                            fns.append(mm(ps[:, bx, 2:4], WH[:, kc, js], xhT[:, xo + kc * 2:xo + kc * 2 + 2], kc == 0, kc == 7))
                    PEG(fns, reads=[WHB] + ([xhTB] if half == 1 else []) + xT_reads(half), writes=[bankB[bh]] + ([bankB[bx]] if half == 1 else []))
                    if j == 1 and half == 1:
                        release(ic)
                        release(ih)
                    o = 1 + half * 512
                    OP(act, lambda e, bc=bc, o=o, hc=hc: e.activation(out=hc.t[:, o:o + 512], in_=ps[:, bc, :], func=AF.Copy),
                       reads=[bankB[bc]], writes=[hc.B])
                    OP(dve, lambda e, bh=bh, o=o, hc=hc: e.tensor_tensor(out=hc.t[:, o:o + 512], in0=ps[:, bh, :], in1=hc.t[:, o:o + 512], op=ALU.mult),
                       reads=[bankB[bh], hc.B], writes=[hc.B])
                    if half == 1:
                        OP(act, lambda e, bx=bx, hc=hc: e.activation(out=hc.t[:, 0:1026:1025], in_=ps[:, bx, 0:2], func=AF.Copy),
                           reads=[bankB[bx]], writes=[hc.B])
                        OP(dve, lambda e, bx=bx, hc=hc: e.tensor_tensor(out=hc.t[:, 0:1026:1025], in0=ps[:, bx, 2:4], in1=hc.t[:, 0:1026:1025], op=ALU.mult),
                           reads=[bankB[bx], hc.B], writes=[hc.B])
                    side_work(st, ca, half)
                OP(act, lambda e, hc=hc, cv=cv, ca=ca: e.activation(out=cv.t[:, 0:1024], in_=hc.t[:, 1:1025], func=AF.Identity, scale=colv[:, CV_CW1 + ca:CV_CW1 + ca + 1]),
                   reads=[hc.B, colvB], writes=[cv.B])
                OP(dve, lambda e, hc=hc, cv=cv, ca=ca: e.scalar_tensor_tensor(out=cv.t[:, 0:1024], in0=hc.t[:, 0:1024], scalar=colv[:, CV_CW0 + ca:CV_CW0 + ca + 1], in1=cv.t[:, 0:1024], op0=ALU.mult, op1=ALU.add),
                   reads=[hc.B, cv.B, colvB], writes=[cv.B])
                OP(dve, lambda e, hc=hc, cv=cv, ca=ca: e.scalar_tensor_tensor(out=cv.t[:, 0:1024], in0=hc.t[:, 2:1026], scalar=colv[:, CV_CW2 + ca:CV_CW2 + ca + 1], in1=cv.t[:, 0:1024], op0=ALU.mult, op1=ALU.add),
                   reads=[hc.B, cv.B, colvB], writes=[cv.B])
                side_work(st, ca, 2)
                for half in range(2):
                    hs = slice(half * 512, (half + 1) * 512)
                    bbk = nb_slow()
                    PEG([mm(ps[:, bbk, :], WB[:, kc, js], xT[:, kc, hs], kc == 0, kc == 7) for kc in range(8)],
                        reads=[WBB] + xT_reads(half), writes=[bankB[bbk]])
                    OP(dve, lambda e, bbk=bbk, cv=cv, ca=ca, hs=hs: e.tensor_tensor(out=a_sb[:, ca, hs], in0=ps[:, bbk, :], in1=cv.t[:, hs], op=ALU.mult),
                       reads=[bankB[bbk], cv.B], writes=[aB[ca][half]])
                if j == 1:
                    release(ib)
                side_work(st, ca, 3)

        def v_unit(st, tt):
            if True:
                ts_ = slice(tt * 128, (tt + 1) * 128)
                b0 = nb2()
                fns = []
                for kc in range(8):
                    fns.append(mm(ps[:, b0, :], xT[:, kc, ts_], bigW[:, kc, 0:512], kc == 0, kc == 7))
                    fns.append(mm(ps[:, b0 + 1, :], xT[:, kc, ts_], bigW[:, kc, 512:1024], kc == 0, kc == 7))
                PEG(fns, reads=[bigWB] + xTB[tt], writes=[bankB[b0], bankB[b0 + 1]])
                gv = gvbuf()
                OP(act, lambda e, b0=b0, gv=gv: e.activation(out=gv.t[:, 0:1024].rearrange("p (a b) -> p a b", a=2), in_=ps[:, b0:b0 + 2, :], func=AF.Gelu_apprx_tanh),
                   reads=[bankB[b0], bankB[b0 + 1]], writes=[gv.B] + r2B[gv.i])
                stt_, sB, rc = ln_stats([gv.t[:, 0:512], gv.t[:, 512:1024]], [gv.B])

                def norm(gv=gv, tt=tt, stt_=stt_, rc=rc, sB=sB):
                    OP(act, lambda e: e.activation(out=vn_sb[:, tt, :], in_=gv.t[:, 0:1024], func=AF.Identity, scale=stt_[:, rc:rc + 1], bias=stt_[:, 21:22]),
                       reads=[gv.B, sB], writes=[vnB[tt]])
                defer(NORM_LAG, norm)

        def su_unit(st, h, half):
            if True:
                iu = plan[st]["U"][h // 2]
                UB, U = blk(iu)
                js = slice((h % 2) * 128, (h % 2 + 1) * 128)
                if True:
                    hs = slice(half * 512, (half + 1) * 512)
                    bs_ = nb()
                    PEG([mm(ps[:, bs_, c * 128:(c + 1) * 128], vn_sb[:, half * 4 + c, h * 128:(h + 1) * 128], wsTb[:, h, :], True, True) for c in range(4)],
                        reads=[wsTbB] + [vnB[half * 4 + c] for c in range(4)], writes=[bankB[bs_]])
                    bu = nb()
                    PEG([mm(ps[:, bu, :], U[:, kc, js], xT[:, kc, hs], kc == 0, kc == 7) for kc in range(8)],
                        reads=[UB] + xT_reads(half), writes=[bankB[bu]])
                    gu = t2()
                    m = t2()
                    OP(act, lambda e, bu=bu, gu=gu: e.activation(out=gu.t[:, :], in_=ps[:, bu, :], func=AF.Gelu_apprx_tanh),
                       reads=[bankB[bu]], writes=[gu.B])
                    OP(dve, lambda e, bs_=bs_, m=m, h=h: e.scalar_tensor_tensor(out=m.t[:, :].rearrange("p (c i) -> p c i", c=4), in0=ps[:, bs_, :].rearrange("p (c i) -> p c i", c=4),
                                                                                scalar=colv[:, CV_VG + h:CV_VG + h + 1], in1=Cc[:, h:h + 1, :].broadcast_to([128, 4, 128]),
                                                                                op0=ALU.mult, op1=ALU.add),
                       reads=[bankB[bs_], CcB, colvB], writes=[m.B])
                    OP(dve, lambda e, m=m, gu=gu, h=h, hs=hs: e.tensor_tensor(out=bb_sb[:, h, hs], in0=m.t[:, :], in1=gu.t[:, :], op=ALU.mult),
                       reads=[m.B, gu.B], writes=[bbB[h][half]])
                if h % 2 == 1 and half == 1:
                    release(iu)

        def stage_YZ(st):
            nbanks[0] = 8
            for oc in range(8):
                g, j = oc // 2, oc % 2
                ipa0, ipa1, ipb, iga, igb = plan[st]["YZ"][g]
                PA0B, PA0 = blk(ipa0)
                PA1B, PA1 = blk(ipa1)
                PBB, PB = blk(ipb)
                GAB, GA = blk(iga)
                GBB, GB = blk(igb)
                js = slice(j * 128, (j + 1) * 128)
                for half in range(2):
                    hs = slice(half * 512, (half + 1) * 512)
                    bya = nb()
                    PEG([mm(ps[:, bya, :], (PA0[:, ca, js] if ca < 8 else PA1[:, ca - 8, js]), a_sb[:, ca, hs], ca == 0, ca == 11) for ca in range(12)],
                        reads=[PA0B, PA1B] + [aB[ca][half] for ca in range(12)], writes=[bankB[bya]])
                    if j == 1 and half == 1:
                        release(ipa0)
                        release(ipa1)
                    bga = nb()
                    PEG([mm(ps[:, bga, :], GA[:, kc, js], xT[:, kc, hs], kc == 0, kc == 7) for kc in range(8)],
                        reads=[GAB] + xT_reads(half), writes=[bankB[bga]])
                    if j == 1 and half == 1:
                        release(iga)
                    byb = nb()
                    PEG([mm(ps[:, byb, :], PB[:, h, js], bb_sb[:, h, hs], h == 0, h == 7) for h in range(8)],
                        reads=[PBB] + [bbB[h][half] for h in range(8)], writes=[bankB[byb]])
                    if j == 1 and half == 1:
                        release(ipb)
                    bgb = nb()
                    PEG([mm(ps[:, bgb, :], GB[:, kc, js], xT[:, kc, hs], kc == 0, kc == 7) for kc in range(8)],
                        reads=[GBB] + xT_reads(half), writes=[bankB[bgb]])
                    if j == 1 and half == 1:
                        release(igb)
                    sga = t2()
                    sgb = t2()
                    OP(act, lambda e, bga=bga, sga=sga, oc=oc: e.activation(out=sga.t[:, :], in_=ps[:, bga, :], func=AF.Sigmoid, bias=colv[:, CV_BGA + oc:CV_BGA + oc + 1]),
                       reads=[bankB[bga], colvB], writes=[sga.B])
                    OP(act, lambda e, bgb=bgb, sgb=sgb, oc=oc: e.activation(out=sgb.t[:, :], in_=ps[:, bgb, :], func=AF.Sigmoid, bias=colv[:, CV_BGB + oc:CV_BGB + oc + 1]),
                       reads=[bankB[bgb], colvB], writes=[sgb.B])
                    OP(dve, lambda e, bya=bya, sga=sga: e.tensor_tensor(out=sga.t[:, :], in0=ps[:, bya, :], in1=sga.t[:, :], op=ALU.mult),
                       reads=[bankB[bya], sga.B], writes=[sga.B])
                    OP(dve, lambda e, byb=byb, sgb=sgb: e.tensor_tensor(out=sgb.t[:, :], in0=ps[:, byb, :], in1=sgb.t[:, :], op=ALU.mult),
                       reads=[bankB[byb], sgb.B], writes=[sgb.B])
                    OP(dve, lambda e, sga=sga, sgb=sgb, oc=oc, hs=hs: e.tensor_tensor(out=z_sb[:, oc, hs], in0=sga.t[:, :], in1=sgb.t[:, :], op=ALU.add),
                       reads=[sga.B, sgb.B], writes=[zB[oc][half]])

        def stage_O(st):
            info = {}

            def head(tt):
                ts_ = slice(tt * 128, (tt + 1) * 128)
                half = tt // 4
                xr = t4()
                r0 = st * T + tt * 128
                DMA(sp, xr.t[:, 0:1024], x_d[r0:r0 + 128, :], xr.ld, writes=[xr.B])
                b0 = nb2()
                fns = []
                for kc in range(8):
                    fns.append(mm(ps[:, b0, :], z_sb[:, kc, ts_], bigW[:, kc, 0:512], kc == 0, kc == 7))
                    fns.append(mm(ps[:, b0 + 1, :], z_sb[:, kc, ts_], bigW[:, kc, 512:1024], kc == 0, kc == 7))
                PEG(fns, reads=[bigWB] + [zB[kc][half] for kc in range(8)], writes=[bankB[b0], bankB[b0 + 1]])
                OP(dve, lambda e: e.scalar_tensor_tensor(out=xr.t[:, 0:1024].rearrange("p (a b) -> p a b", a=2), in0=xr.t[:, 0:1024].rearrange("p (a b) -> p a b", a=2),
                                                         scalar=ALPHA, in1=ps[:, b0:b0 + 2, :], op0=ALU.mult, op1=ALU.add),
                   reads=[bankB[b0], bankB[b0 + 1], xr.B], writes=[xr.B])
                stt_, sB, rc = ln_stats([xr.t[:, 0:512], xr.t[:, 512:1024]], [xr.B])
                info[tt] = (xr, stt_, sB, rc)

            def norm(tt):
                xr, stt_, sB, rc = info[tt]
                OP(act, lambda e: e.activation(out=xr.t[:, 0:1024], in_=xr.t[:, 0:1024], func=AF.Identity, scale=stt_[:, rc:rc + 1], bias=stt_[:, 21:22]),
                   reads=[xr.B, sB], writes=[xr.B])

            def tail(tt):
                xr = info[tt][0]
                ts_ = slice(tt * 128, (tt + 1) * 128)
                b0 = nb2()
                PEG([tr(ps[:, b0 + kc // 4, (kc % 4) * 128:(kc % 4 + 1) * 128], xr.t[:, kc * 128:(kc + 1) * 128]) for kc in range(8)],
                    reads=[xr.B, memB], writes=[bankB[b0], bankB[b0 + 1]], do_tick=False)
                for kc in range(4):
                    src = ps[:, b0, kc * 128:(kc + 1) * 128]
                    OP(act, lambda e, src=src, kc=kc: e.activation(out=x1Tf[:, kc, ts_], in_=src, func=AF.Identity, scale=colv[:, CV_L1G + kc:CV_L1G + kc + 1], bias=colv[:, CV_L1B + kc:CV_L1B + kc + 1]),
                       reads=[bankB[b0], colvB], writes=[x1fB[tt][0]])
                g1b = colv[:, CV_L1G + 4:CV_L1G + 8].unsqueeze(2).broadcast_to([128, 4, 128])
                b1b = colv[:, CV_L1B + 4:CV_L1B + 8].unsqueeze(2).broadcast_to([128, 4, 128])
                OP(dve, lambda e: e.tensor_tensor(out=x1Tf[:, 4:8, ts_], in0=ps[:, b0 + 1, :].rearrange("p (k n) -> p k n", k=4), in1=g1b, op=ALU.mult),
                   reads=[bankB[b0 + 1], colvB], writes=[x1fB[tt][1]])
                OP(dve, lambda e: e.tensor_tensor(out=x1Tf[:, 4:8, ts_], in0=x1Tf[:, 4:8, ts_], in1=b1b, op=ALU.add),
                   reads=[x1fB[tt][1], colvB], writes=[x1fB[tt][1]])

            def cast(tt):
                ts_ = slice(tt * 128, (tt + 1) * 128)
                OP(act, lambda e: e.activation(out=x1Tb[:, :, ts_], in_=x1Tf[:, :, ts_], func=AF.Copy), reads=x1fB[tt], writes=[x1bB[tt]])

            def step(k):
                if 0 <= k - 3 < 8:
                    cast(k - 3)
                if k < 8:
                    head(k)
                if 0 <= k - 1 < 8:
                    norm(k - 1)
                if 0 <= k - 2 < 8:
                    tail(k - 2)

            for k in range(8):
                step(k)
            if st + 1 < ST:
                DMA(pool, bigW[:], winr[:, :, OFF_VB:OFF_VB + 1024], bigW_sem, writes=[bigWB])
            return [lambda k=k: step(k) for k in range(8, 11)]

        def f1_group(st, fc, half):
            i1 = plan[st]["F1"][fc // 2]
            WB_, W = blk(i1)
            js = slice((fc % 2) * 128, (fc % 2 + 1) * 128)
            hs = slice(half * 512, (half + 1) * 512)
            b = nb()
            PEG([mm(ps[:, b, :], W[:, kc, js], x1Tb[:, kc, hs], kc == 0, kc == 7) for kc in range(8)],
                reads=[WB_] + [x1bB[tt] for tt in range(half * 4, half * 4 + 4)], writes=[bankB[b]])
            r = t2()
            OP(act, lambda e, b=b, r=r: e.activation(out=r.t[:, :], in_=ps[:, b, :], func=AF.Relu), reads=[bankB[b]], writes=[r.B])
            OP(dve, lambda e, r=r, fc=fc, hs=hs: e.tensor_tensor(out=hid[:, fc, hs], in0=r.t[:, :], in1=r.t[:, :], op=ALU.mult),
               reads=[r.B], writes=[hidB[fc][half]] + ([NX[0].B, NX[1].B] if fc >= 28 else []))

        def bigq_views():
            bigv = bigW[:].rearrange("p k n -> p (k n)")
            return [bigv[:, q * 2048:(q + 1) * 2048].rearrange("p (k n) -> p k n", k=8) for q in range(4)]

        def stage_F1(st, drain):
            G0 = 6
            if st == ST - 1:
                bq = bigq_views()
                for q in range(4):
                    DMA(pool, bq[q], wf2r[:, q * 8:(q + 1) * 8, 0:256], bigW_sem, writes=[bigWB])
            n = 0
            for g in range(G0):
                for fc in (2 * g, 2 * g + 1):
                    f1_group(st, fc, 0)
                    n += 1
                    if n % 2 == 0 and drain:
                        drain.pop(0)()
            while drain:
                drain.pop(0)()
            flush_deferred()
            for g in range(G0):
                for fc in (2 * g, 2 * g + 1):
                    f1_group(st, fc, 1)
                release(plan[st]["F1"][g])
            for g in range(G0, 16):
                for fc in (2 * g, 2 * g + 1):
                    f1_group(st, fc, 0)
                    f1_group(st, fc, 1)
                release(plan[st]["F1"][g])

        out_toks = []
        pending_tails = []

        def ln2_tail(st, tt, final=False):
            half = tt // 4
            ts_ = slice(tt * 128, (tt + 1) * 128)
            b0 = nb2()
            PEG([tr(ps[:, b0 + oc // 4, (oc % 4) * 128:(oc % 4 + 1) * 128], x1Tf[:, oc, ts_]) for oc in range(8)],
                reads=[memB] + [r2B[oc][tt // 2] for oc in range(8)], writes=[bankB[b0], bankB[b0 + 1]], do_tick=False)
            if st < ST - 1:
                k = nxc[0] % 3
                nxc[0] += 1
                n2 = NX[k] if k < 2 else t4()
            else:
                n2 = t4()
            if final:
                OP(act, lambda e: e.activation(out=n2.t[:, 0:1024].rearrange("p (a b) -> p a b", a=2), in_=ps[:, b0:b0 + 2, :], func=AF.Copy),
                   reads=[bankB[b0], bankB[b0 + 1]], writes=[n2.B])
                stt_, sB, rc = ln_stats([n2.t[:, 0:512], n2.t[:, 512:1024]], [n2.B])
            else:
                stt_, sB = nstat()
                rc = 15
                OP(pool, lambda e: e.memset(stt_[:, 0:2], 0.0), writes=[sB])
                OP(act, lambda e: e.activation(out=n2.t[:, 0:1024].rearrange("p (a b) -> p a b", a=2), in_=ps[:, b0:b0 + 2, :], func=AF.Copy, accum_out=stt_[:, 0:1]),
                   reads=[bankB[b0], bankB[b0 + 1]], writes=[n2.B, sB])
                OP(act, lambda e: e.activation(out=ps[:, b0:b0 + 2, :], in_=ps[:, b0:b0 + 2, :], func=AF.Square, accum_out=stt_[:, 1:2]),
                   reads=[bankB[b0], bankB[b0 + 1], sB], writes=[bankB[b0], bankB[b0 + 1], sB])
                bankB[b0].open = False
                bankB[b0 + 1].open = False
                OP(pool, lambda e: e.tensor_scalar(out=stt_[:, 12:14], in0=stt_[:, 0:2], scalar1=1.0 / 1024.0, scalar2=1.0, op0=ALU.mult, op1=ALU.mult), reads=[sB], writes=[sB])
                OP(pool, lambda e: e.tensor_tensor(out=stt_[:, 2:3], in0=stt_[:, 12:13], in1=stt_[:, 12:13], op=ALU.mult), reads=[sB], writes=[sB])
                OP(pool, lambda e: e.tensor_tensor(out=stt_[:, 3:4], in0=stt_[:, 13:14], in1=stt_[:, 2:3], op=ALU.subtract), reads=[sB], writes=[sB])
                OP(pool, lambda e: e.tensor_scalar(out=stt_[:, 14:15], in0=stt_[:, 3:4], scalar1=LN_EPS, scalar2=1.0, op0=ALU.add, op1=ALU.mult), reads=[sB], writes=[sB])
                OP(pool, lambda e: e.tensor_tensor(out=stt_[:, 15:16], in0=stt_[:, 14:15], in1=chalf[:, 0:1], op=ALU.pow), reads=[sB, memB], writes=[sB])
                OP(pool, lambda e: e.tensor_scalar(out=stt_[:, 21:22], in0=stt_[:, 12:13], scalar1=stt_[:, 15:16], scalar2=-1.0, op0=ALU.mult, op1=ALU.mult), reads=[sB], writes=[sB])

            def phase2():
                OP(act, lambda e: e.activation(out=n2.t[:, 0:1024], in_=n2.t[:, 0:1024], func=AF.Identity, scale=stt_[:, rc:rc + 1], bias=stt_[:, 21:22]),
                   reads=[n2.B, sB], writes=[n2.B])
                OP(dve, lambda e: e.tensor_tensor(out=n2.t[:, 0:1024], in0=n2.t[:, 0:1024], in1=g2rep[:, :], op=ALU.mult), reads=[n2.B, g2B], writes=[n2.B])
                OP(dve, lambda e: e.tensor_tensor(out=n2.t[:, 0:1024], in0=n2.t[:, 0:1024], in1=b2rep[:, :], op=ALU.add), reads=[n2.B, b2B], writes=[n2.B])
                r0 = st * T + tt * 128
                tok = DMA(sp, y_d[r0:r0 + 128, :], n2.t[:, 0:1024], n2.st, reads=[n2.B])
                out_toks.append(tok)
            defer(NORM_LAG, phase2)

        def queue_tails(st, tts=tuple(range(8))):
            final = (st == ST - 1 and tts[0] >= 4)
            for tt in tts:
                pending_tails.append(lambda st=st, tt=tt: ln2_tail(st, tt, final))

        def f2_group(st, Ws, oc, j, half, quarter=None):
            js = slice(j * 128, (j + 1) * 128)
            if quarter is None:
                hs = slice(half * 512, (half + 1) * 512)
                qs = [2 * half, 2 * half + 1]
                tts = range(half * 4, half * 4 + 4)
                n = 512
            else:
                hs = slice(half * 512 + quarter * 256, half * 512 + (quarter + 1) * 256)
                qs = [2 * half + quarter]
                tts = range(half * 4 + quarter * 2, half * 4 + quarter * 2 + 2)
                n = 256
            b = nb()
            PEG([mm(ps[:, b, 0:n], Ws[fc // 8][1][:, fc % 8, js], hid[:, fc, hs], fc == 0, fc == 31) for fc in range(32)],
                reads=[w[0] for w in Ws] + [hidB[fc][half] for fc in range(32)], writes=[bankB[b]])
            OP(dve, lambda e, b=b, oc=oc, hs=hs, n=n: e.scalar_tensor_tensor(out=x1Tf[:, oc, hs], in0=x1Tf[:, oc, hs], scalar=ALPHA, in1=ps[:, b, 0:n], op0=ALU.mult, op1=ALU.add),
               reads=[bankB[b]] + [x1fB[tt][oc // 4] for tt in tts], writes=[r2B[oc][q] for q in qs])

        def stage_F2(st):
            if st < ST - 1:
                load_xT(st + 1, 0)
                load_xT(st + 1, 1)
                for g in range(4):
                    ids = plan[st]["F2"][g]
                    Ws = [blk(i) for i in ids]
                    for j in range(2):
                        for half in range(2):
                            f2_group(st, Ws, g * 2 + j, j, half)
                    for i in ids:
                        release(i)
                queue_tails(st)
            else:
                passes = [("F2", 0, None, (0, 1, 2, 3)), ("F2b", 1, None, (4, 5, 6, 7))]
                bigq = bigq_views()
                for key, half, quarter, tts in passes:
                    for g in range(4):
                        ids = plan[st][key][g]
                        if ids is None:
                            Ws = [(bigWB, bigq[q]) for q in range(4)]
                        else:
                            Ws = [blk(i) for i in ids]
                        for j in range(2):
                            f2_group(st, Ws, g * 2 + j, j, half, quarter)
                            if pending_tails and half == 1 and j == 0:
                                pending_tails.pop(0)()
                        for i in (ids or []):
                            release(i)
                    queue_tails(st, tts)

        setup_Cc()
        for st in range(ST):
            stage_A(st)
            stage_YZ(st)
            drain = stage_O(st)
            stage_F1(st, drain)
            stage_F2(st)
        flush_deferred()
        while pending_tails:
            pending_tails.pop(0)()
        flush_deferred()
        sp.wait(out_toks)

        with nc.Block() as block:
            @block.tensor
            def _(e):
                for it in pe.items:
                    it(e)

            @block.scalar
            def _(e):
                for it in act.items:
                    it(e)

            @block.vector
            def _(e):
                for it in dve.items:
                    it(e)

            @block.gpsimd
            def _(e):
                for it in pool.items:
                    it(e)

            @block.sync
            def _(e):
                for it in sp.items:
                    it(e)
    return nc


_NC_CACHE = {}


def _host_layout(x, conv_w, b_gate, v_norm_g, v_norm_b, w_s, b_s, ln1_g, ln1_b):
    xs = np.ascontiguousarray(x[0])
    cvs = np.concatenate([
        conv_w[0].reshape(3, 12, 128).transpose(2, 0, 1).reshape(128, 36),
        b_gate[0].reshape(16, 128).T,
        v_norm_g[0].reshape(8, 128).T,
        v_norm_b[0].reshape(8, 128).T,
        ln1_g[0].reshape(8, 128).T,
        ln1_b[0].reshape(8, 128).T,
    ], axis=1)
    cvs = np.ascontiguousarray(cvs, dtype=np.float32)
    assert cvs.shape == (128, NCV)
    wsT = np.ascontiguousarray(w_s[0].transpose(2, 0, 1).reshape(128, 1024))
    bsrow = np.ascontiguousarray(b_s[0].reshape(1, 1024))
    xpad = np.concatenate([np.zeros((1, D), np.float32), xs, np.zeros((1, D), np.float32)], axis=0)
    per_core = []
    for c in range(NCORES):
        xh = np.zeros((ST, 2, D), np.float32)
        for st in range(ST):
            g0 = c * TOK + st * T
            xh[st, 0] = xpad[g0]
            xh[st, 1] = xpad[g0 + T + 1]
        xhT = np.ascontiguousarray(xh.reshape(ST, 2, 8, 128).transpose(3, 0, 2, 1).reshape(128, ST * 16))
        xc = np.ascontiguousarray(xs[c * TOK:(c + 1) * TOK])
        per_core.append((xc, xhT, np.ascontiguousarray(xc.T)))
    return cvs, wsT, bsrow, per_core


def kernel(x, w_in, b_gate, conv_w, v_norm_g, v_norm_b, w_s, b_s, w_pa, w_pb, w_o,
           ln1_g, ln1_b, w_ff1, w_ff2, ln2_g, ln2_b):
    f = lambda a: np.ascontiguousarray(np.asarray(a, dtype=np.float32))
    x, w_in, b_gate, conv_w, v_norm_g, v_norm_b, w_s, b_s = map(f, (x, w_in, b_gate, conv_w, v_norm_g, v_norm_b, w_s, b_s))
    w_pa, w_pb, w_o, ln1_g, ln1_b, w_ff1, w_ff2, ln2_g, ln2_b = map(f, (w_pa, w_pb, w_o, ln1_g, ln1_b, w_ff1, w_ff2, ln2_g, ln2_b))
    cvs, wsT, bsrow, per_core = _host_layout(x, conv_w, b_gate, v_norm_g, v_norm_b, w_s, b_s, ln1_g, ln1_b)
    if "nc" not in _NC_CACHE:
        _NC_CACHE["nc"] = build_program()
    nc = _NC_CACHE["nc"]
    shared = {
        "w_in": w_in[0], "w_pa": w_pa[0], "w_pb": w_pb[0], "w_o": w_o[0], "w_ff1": w_ff1[0], "w_ff2": w_ff2[0],
        "colvecs": cvs, "wsT": wsT, "bsrow": bsrow, "ln2g": ln2_g.reshape(1, 1024), "ln2b": ln2_b.reshape(1, 1024),
    }
    in_maps = []
    for c in range(NCORES):
        m = dict(shared)
        m["x"] = per_core[c][0]
        m["xhT"] = per_core[c][1]
        m["xT"] = per_core[c][2]
        in_maps.append(m)
    res = run_bass_kernel_spmd(nc, in_maps, core_ids=list(range(NCORES)))
    out = np.concatenate([r["y"] for r in res.results], axis=0)
    return out.reshape(1, SEQ, D).astype(np.float32)
```

```python
import numpy as np
import concourse.bass as bass
import concourse.mybir as mybir
from concourse.bass_utils import run_bass_kernel_spmd

F32 = mybir.dt.float32
BF16 = mybir.dt.bfloat16
I32 = mybir.dt.int32
AF = mybir.ActivationFunctionType
ALU = mybir.AluOpType

NCORES = 8
D = 1024
SEQ = 16384
TOK = SEQ // NCORES
T = 1024
ST = TOK // T
W_A = 1536
D_FF = 4096
N_PROJ = 8704
OFF_BA, OFF_CA, OFF_HA, OFF_UB, OFF_VB, OFF_GA, OFF_GB = 0, 1536, 3072, 4608, 5632, 6656, 7680
LN_EPS = 1e-5
ALPHA = float(2.0 ** 0.25)
NS = 8
NT4 = 5
NT2 = 4
N_WARM = 36
NORM_LAG = 5
CV_CW0, CV_CW1, CV_CW2, CV_BGA, CV_BGB, CV_VG, CV_VB, CV_L1G, CV_L1B = 0, 12, 24, 36, 44, 52, 60, 68, 76
NCV = 84


class Buf:
    def __init__(self, accum=False):
        self.w = {}
        self.r = {}
        self.accum = accum


def _merge(d, tok):
    k = id(tok[0])
    if k not in d or d[k][1] < tok[1]:
        d[k] = tok


def deps(reads, writes):
    toks = []
    for b in reads:
        toks += list(b.w.values())
    for b in writes:
        if b.accum:
            continue
        toks += list(b.w.values())
        toks += list(b.r.values())
    return toks


def commit(tok, reads, writes):
    for b in reads:
        _merge(b.r, tok)
    for b in writes:
        if b.accum:
            _merge(b.w, tok)
        else:
            b.w = {id(tok[0]): tok}
            b.r = {}


class Q:
    def __init__(self, name, sem):
        self.name = name
        self.sem = sem
        self.cnt = 0
        self.seen = {}
        self.items = []

    def wait(self, toks):
        need = {}
        for t in toks:
            sem, v = t
            if self.name == "pe" and sem is self.sem:
                continue
            k = id(sem)
            if self.seen.get(k, 0) >= v:
                continue
            if k not in need or need[k][1] < v:
                need[k] = (sem, v)
        for sem, v in need.values():
            self.seen[id(sem)] = v
            self.items.append(lambda e, sem=sem, v=v: e.wait_ge(sem, v))

    def run(self, fn):
        self.cnt += 1
        sem = self.sem
        self.items.append(lambda e, fn=fn, sem=sem: fn(e).then_inc(sem, 1))
        return (sem, self.cnt)

    def run_nomark(self, fn):
        self.items.append(lambda e, fn=fn: fn(e))

    def dma(self, out, in_, semst):
        semst[1] += 16
        sem = semst[0]
        self.items.append(lambda e, out=out, in_=in_, sem=sem: e.dma_start(out=out, in_=in_).then_inc(sem, 16))
        return (sem, semst[1])


def build_program():
    nc = bass.Bass("TRN2", target_bir_lowering=False)
    x_d = nc.dram_tensor("x", [TOK, D], F32, kind="ExternalInput").ap()
    xh_d = nc.dram_tensor("xhT", [128, ST * 16], F32, kind="ExternalInput").ap()
    xTd = nc.dram_tensor("xT", [D, TOK], F32, kind="ExternalInput").ap()
    win_d = nc.dram_tensor("w_in", [D, N_PROJ], F32, kind="ExternalInput").ap()
    wpa_d = nc.dram_tensor("w_pa", [W_A, D], F32, kind="ExternalInput").ap()
    wpb_d = nc.dram_tensor("w_pb", [D, D], F32, kind="ExternalInput").ap()
    wo_d = nc.dram_tensor("w_o", [D, D], F32, kind="ExternalInput").ap()
    wf1_d = nc.dram_tensor("w_ff1", [D, D_FF], F32, kind="ExternalInput").ap()
    wf2_d = nc.dram_tensor("w_ff2", [D_FF, D], F32, kind="ExternalInput").ap()
    cv_d = nc.dram_tensor("colvecs", [128, NCV], F32, kind="ExternalInput").ap()
    wsT_d = nc.dram_tensor("wsT", [128, 1024], F32, kind="ExternalInput").ap()
    bs_d = nc.dram_tensor("bsrow", [1, 1024], F32, kind="ExternalInput").ap()
    g2_d = nc.dram_tensor("ln2g", [1, 1024], F32, kind="ExternalInput").ap()
    b2_d = nc.dram_tensor("ln2b", [1, 1024], F32, kind="ExternalInput").ap()
    y_d = nc.dram_tensor("y", [TOK, D], F32, kind="ExternalOutput").ap()

    xTr = xTd.rearrange("(k p) n -> p k n", p=128)
    winr = win_d.rearrange("(k p) n -> p k n", p=128)
    wpar = wpa_d.rearrange("(k p) n -> p k n", p=128)
    wpbr = wpb_d.rearrange("(k p) n -> p k n", p=128)
    wor = wo_d.rearrange("(k p) n -> p k n", p=128)
    wf1r = wf1_d.rearrange("(k p) n -> p k n", p=128)
    wf2r = wf2_d.rearrange("(k p) n -> p k n", p=128)

    from contextlib import ExitStack
    es = ExitStack()
    with es:
        def sb(name, shape, dt):
            return es.enter_context(nc.sbuf_tensor(name, shape, dt))

        def semaphore(name):
            return es.enter_context(nc.semaphore(name))

        regP = sb("regP", [128, 32768], BF16)
        regQ = sb("regQ", [128, 8192], F32)
        regR = sb("regR", [128, 8, T], BF16)
        ring = sb("ring", [128, NS, 2048], BF16)
        bigW = sb("bigW", [128, 8, 1024], BF16)
        t4t = [sb(f"t4_{i}", [128, 1032], F32) for i in range(NT4)]
        t2t = [sb(f"t2_{i}", [128, 512], F32) for i in range(NT2)]
        Cc = sb("Cc", [128, 8, 128], F32)
        g2rep = sb("g2rep", [128, 1024], F32)
        b2rep = sb("b2rep", [128, 1024], F32)
        wsTb = sb("wsTb", [128, 8, 128], BF16)
        ident = sb("ident", [128, 128], F32)
        ones = sb("ones", [128, 128], F32)
        colv = sb("colv", [128, NCV], F32)
        chalf = sb("chalf", [128, 2], F32)
        xhT = sb("xhTs", [128, ST * 16], BF16)
        NSTAT = 6
        statt = [sb(f"stat{i}", [128, 32], F32) for i in range(NSTAT)]
        ps = es.enter_context(nc.psum_tensor("ps", [128, 8, 512], F32))

        a_sb = regP[:, 0:12288].rearrange("p (k n) -> p k n", k=12)
        bb_sb = regP[:, 12288:20480].rearrange("p (k n) -> p k n", k=8)
        vn_sb = regP[:, 20480:28672].rearrange("p (k n) -> p k n", k=8)
        z_sb = regP[:, 20480:28672].rearrange("p (k n) -> p k n", k=8)
        hid = regP[:, 0:32768].rearrange("p (k n) -> p k n", k=32)
        xT = regR
        x1Tb = regR
        x1Tf = regQ[:, 0:8192].rearrange("p (k n) -> p k n", k=8)

        pe = Q("pe", semaphore("s_pe"))
        act = Q("act", semaphore("s_act"))
        dve = Q("dve", semaphore("s_dve"))
        pool = Q("pool", semaphore("s_pool"))
        sp = Q("sp", semaphore("s_sp"))

        def OP(q, fn, reads=(), writes=()):
            for b in reads:
                b.open = False
            q.wait(deps(reads, writes))
            tok = q.run(fn)
            commit(tok, reads, writes)
            return tok

        deferred = []

        def tick():
            ready = []
            for d in deferred:
                d[0] -= 1
                if d[0] <= 0:
                    ready.append(d)
            for d in ready:
                deferred.remove(d)
            for d in ready:
                d[1]()

        def defer(n, fn):
            deferred.append([n, fn])

        def flush_deferred():
            while deferred:
                d = deferred.pop(0)
                d[1]()

        def PEG(fns, reads, writes, do_tick=True):
            pe.wait(deps(reads, writes))
            for f in fns[:-1]:
                pe.run_nomark(f)
            tok = pe.run(fns[-1])
            commit(tok, reads, writes)
            for b in writes:
                b.open = True
            if do_tick:
                tick()
            return tok

        def DMA(q, out, in_, semst, reads=(), writes=()):
            q.wait(deps(reads, writes))
            tok = q.dma(out, in_, semst)
            commit(tok, reads, writes)
            return tok

        def mm(out, l, r, st, sp_):
            return lambda e: e.matmul(out, lhsT=l, rhs=r, start=st, stop=sp_)

        def tr(out, in_):
            return lambda e: e.transpose(out=out, in_=in_, identity=ident[:])

        bankB = [Buf() for _ in range(8)]
        cur = [0]

        nbanks = [8]
        slowc = [0]

        def nb():
            i = cur[0] % nbanks[0]
            cur[0] = (i + 1) % nbanks[0]
            assert not getattr(bankB[i], "open", False), "PSUM bank re-allocated before its reader was emitted"
            return i

        def nb_slow():
            i = 6 + slowc[0] % 2
            slowc[0] += 1
            assert not getattr(bankB[i], "open", False), "PSUM bank re-allocated before its reader was emitted"
            return i

        def nb2():
            c = cur[0] % nbanks[0]
            if c % 2:
                c = (c + 1) % nbanks[0]
            i = c
            cur[0] = (i + 2) % nbanks[0]
            assert not getattr(bankB[i], "open", False) and not getattr(bankB[i + 1], "open", False), "PSUM bank re-allocated before its reader was emitted"
            return i

        class Tmp:
            def __init__(self, t, name):
                self.t = t
                self.B = Buf()
                self.ld = [semaphore("ld_" + name), 0]
                self.st = [semaphore("st_" + name), 0]

        T4 = [Tmp(t, f"t4_{i}") for i, t in enumerate(t4t)]
        NX = [Tmp(regP[:, 28672 + i * 2048:28672 + (i + 1) * 2048].bitcast(F32), f"nx_{i}") for i in range(2)]
        nxc = [0]
        T2 = [Tmp(t, f"t2_{i}") for i, t in enumerate(t2t)]
        c4 = [0]
        c2 = [0]
        cs = [0]

        def t4():
            r = T4[c4[0] % NT4]
            c4[0] += 1
            return r

        def t2():
            r = T2[c2[0] % NT2]
            c2[0] += 1
            return r

        class GvTmp:
            def __init__(self, i):
                self.t = regQ[:, i * 1024:(i + 1) * 1024]
                self.B = Buf()
                self.i = i

        GV = [GvTmp(i) for i in range(8)]
        cg = [0]

        def gvbuf():
            r = GV[cg[0] % 8]
            cg[0] += 1
            return r

        statB = [Buf() for _ in range(NSTAT)]

        def nstat():
            i = cs[0] % NSTAT
            cs[0] += 1
            return statt[i], statB[i]

        xTB = [[Buf(), Buf()] for _ in range(8)]
        aB = [[Buf(), Buf()] for _ in range(12)]
        vnB = [Buf() for _ in range(8)]
        bbB = [[Buf(), Buf()] for _ in range(8)]
        zB = [[Buf(), Buf()] for _ in range(8)]
        x1fB = [[Buf(), Buf()] for _ in range(8)]
        x1bB = [Buf() for _ in range(8)]
        r2B = [[Buf() for _ in range(4)] for _ in range(8)]
        hidB = [[Buf(), Buf()] for _ in range(32)]
        memB = Buf(accum=True)
        colvB = Buf(accum=True)
        xhTB = Buf(accum=True)
        wsTbB = Buf(accum=True)
        CcB = Buf(accum=True)
        g2B = Buf(accum=True)
        b2B = Buf(accum=True)
        bigWB = Buf()
        bigW_sem = [semaphore("bigw"), 0]
        sem_xh = [semaphore("ld_xh"), 0]
        sem_ws = [semaphore("ld_ws"), 0]
        sem_cv = [semaphore("ld_cv"), 0]
        sem_g2 = [semaphore("ld_g2"), 0]
        sem_b2 = [semaphore("ld_b2"), 0]
        slotB = [Buf() for _ in range(NS)]
        slot_sem = [[semaphore(f"slot{i}"), 0] for i in range(NS)]

        blocks = []

        def add_block(src, nk, ncol):
            blocks.append((src, nk, ncol))
            return len(blocks) - 1

        plan = []
        for st in range(ST):
            p = {}
            p["A"] = [None] * 6
            p["U"] = [None] * 4

            def addA(g):
                c = add_block(winr[:, :, OFF_CA + g * 256:OFF_CA + (g + 1) * 256], 8, 256)
                h = add_block(winr[:, :, OFF_HA + g * 256:OFF_HA + (g + 1) * 256], 8, 256)
                b = add_block(winr[:, :, OFF_BA + g * 256:OFF_BA + (g + 1) * 256], 8, 256)
                p["A"][g] = (c, h, b)

            def addU(g):
                p["U"][g] = add_block(winr[:, :, OFF_UB + g * 256:OFF_UB + (g + 1) * 256], 8, 256)

            for g in range(5):
                addA(g)
            addU(0)
            addU(1)
            addA(5)
            addU(2)
            addU(3)
            p["YZ"] = []
            for g in range(4):
                pa0 = add_block(wpar[:, 0:8, g * 256:(g + 1) * 256], 8, 256)
                pa1 = add_block(wpar[:, 8:12, g * 256:(g + 1) * 256], 4, 256)
                pb = add_block(wpbr[:, :, g * 256:(g + 1) * 256], 8, 256)
                ga = add_block(winr[:, :, OFF_GA + g * 256:OFF_GA + (g + 1) * 256], 8, 256)
                gb = add_block(winr[:, :, OFF_GB + g * 256:OFF_GB + (g + 1) * 256], 8, 256)
                p["YZ"].append((pa0, pa1, pb, ga, gb))
            p["F1"] = [add_block(wf1r[:, :, g * 256:(g + 1) * 256], 8, 256) for g in range(16)]
            p["F2"] = []
            for g in range(4):
                p["F2"].append([add_block(wf2r[:, q * 8:(q + 1) * 8, g * 256:(g + 1) * 256], 8, 256) for q in range(4)])
            if st == ST - 1:
                for key in ("F2b",):
                    p[key] = ["bigW", "regR"]
                    for g in range(2, 4):
                        p[key].append([add_block(wf2r[:, q * 8:(q + 1) * 8, g * 256:(g + 1) * 256], 8, 256) for q in range(4)])
            plan.append(p)
        issued = set()
        released = set()

        def issue(i):
            assert i not in issued
            issued.add(i)
            src, nk, ncol = blocks[i]
            s = i % NS
            DMA(pool, ring[:, s, 0:nk * ncol].rearrange("p (k n) -> p k n", k=nk), src, slot_sem[s], writes=[slotB[s]])

        def blk(i):
            assert i in issued and (i + NS) not in issued, i
            src, nk, ncol = blocks[i]
            s = i % NS
            return slotB[s], ring[:, s, 0:nk * ncol].rearrange("p (k n) -> p k n", k=nk)

        def release(i):
            assert i not in released
            released.add(i)
            j = i + NS
            if j < len(blocks):
                issue(j)

        def ln_stats(src_aps, srcbufs):
            stt_, sB = nstat()
            for i, ap in enumerate(src_aps):
                OP(dve, lambda e, ap=ap, i=i: e.bn_stats(out=stt_[:, i * 6:(i + 1) * 6], in_=ap), reads=srcbufs, writes=[sB])
            n = len(src_aps)
            OP(dve, lambda e: e.bn_aggr(out=stt_[:, 12:14], in_=stt_[:, 0:6 * n]), reads=[sB], writes=[sB])
            OP(pool, lambda e: e.tensor_scalar(out=stt_[:, 14:15], in0=stt_[:, 13:14], scalar1=LN_EPS, scalar2=1.0, op0=ALU.add, op1=ALU.mult), reads=[sB], writes=[sB])
            OP(pool, lambda e: e.tensor_tensor(out=stt_[:, 15:16], in0=stt_[:, 14:15], in1=chalf[:, 0:1], op=ALU.pow), reads=[sB, memB], writes=[sB])
            OP(pool, lambda e: e.tensor_scalar(out=stt_[:, 21:22], in0=stt_[:, 12:13], scalar1=stt_[:, 15:16], scalar2=-1.0, op0=ALU.mult, op1=ALU.mult), reads=[sB], writes=[sB])
            return stt_, sB, 15

        OP(pool, lambda e: e.memset(ident[:], 0.0), writes=[memB])
        OP(pool, lambda e: e.affine_select(out=ident[:], in_=ident[:], compare_op=ALU.not_equal, fill=1.0, base=0, pattern=[[-1, 128]], channel_multiplier=1), reads=[memB], writes=[memB])
        OP(pool, lambda e: e.memset(ones[:], 1.0), writes=[memB])
        OP(pool, lambda e: e.memset(chalf[:], -0.5), writes=[memB])
        xT_sem = [[semaphore("xT0"), 0], [semaphore("xT1"), 0]]

        def load_xT(st, half):
            tts = range(half * 4, half * 4 + 4)
            wr = [xTB[tt][k] for tt in tts for k in range(2)] + [x1bB[tt] for tt in tts]
            c0 = st * T + half * 512
            DMA(pool, xT[:, :, half * 512:(half + 1) * 512], xTr[:, :, c0:c0 + 512], xT_sem[half], writes=wr)

        def setup_Cc():
            wsf = t4()
            bsr = t4()
            DMA(sp, wsf.t[:, 0:1024], wsT_d, wsf.ld, writes=[wsf.B])
            DMA(sp, bsr.t[:, 0:1024], bs_d.broadcast_to([128, 1024]), bsr.ld, writes=[bsr.B])
            b0 = nb2()
            PEG([mm(ps[:, b0 + h // 4, (h % 4) * 128:(h % 4 + 1) * 128], ones[:], wsf.t[:, h * 128:(h + 1) * 128], True, True) for h in range(8)],
                reads=[memB, wsf.B], writes=[bankB[b0], bankB[b0 + 1]], do_tick=False)
            PEG([mm(ps[:, 7, 0:128], ones[:], ones[:], True, True) for _ in range(N_WARM)],
                reads=[memB], writes=[bankB[7]], do_tick=False)
            bankB[7].open = False
            load_xT(0, 0)
            issue(0)
            DMA(sp, colv[:], cv_d, sem_cv, writes=[colvB])
            pool.wait(list(slotB[0].w.values()) + list(xTB[0][0].w.values()))
            issue(1)
            load_xT(0, 1)
            DMA(pool, xhT[:], xh_d, sem_xh, writes=[xhTB])
            pool.wait(list(slotB[1].w.values()) + list(xTB[4][0].w.values()))
            issue(2)
            DMA(pool, wsTb[:], wsT_d.rearrange("p (h i) -> p h i", h=8), sem_ws, writes=[wsTbB])
            for i in range(3, NS):
                issue(i)
            DMA(pool, bigW[:], winr[:, :, OFF_VB:OFF_VB + 1024], bigW_sem, writes=[bigWB])
            DMA(sp, g2rep[:], g2_d.broadcast_to([128, 1024]), sem_g2, writes=[g2B])
            DMA(sp, b2rep[:], b2_d.broadcast_to([128, 1024]), sem_b2, writes=[b2B])
            for h in range(8):
                OP(dve, lambda e, h=h: e.scalar_tensor_tensor(out=Cc[:, h, :], in0=ps[:, b0 + h // 4, (h % 4) * 128:(h % 4 + 1) * 128],
                                                             scalar=colv[:, CV_VB + h:CV_VB + h + 1], in1=bsr.t[:, h * 128:(h + 1) * 128],
                                                             op0=ALU.mult, op1=ALU.add),
                   reads=[bankB[b0 + h // 4], bsr.B, colvB], writes=[CcB])

        def xT_reads(half):
            r = []
            for tt in range(half * 4, half * 4 + 4):
                r += xTB[tt]
            return r

        def side_work(st, ca, slot):
            if ca < 4:
                if slot in (0, 2) and pending_tails:
                    pending_tails.pop(0)()
            elif ca < 8:
                if slot in (0, 2):
                    v_unit(st, 2 * (ca - 4) + slot // 2)
                if ca == 7 and slot == 3:
                    DMA(pool, bigW[:], wor[:, :, :], bigW_sem, writes=[bigWB])
            else:
                if ca == 8 and slot == 0:
                    flush_deferred()
                su_unit(st, 2 * (ca - 8) + slot // 2, slot % 2)

        def stage_A(st):
            nbanks[0] = 6
            for ca in range(12):
                g, j = ca // 2, ca % 2
                ic, ih, ib = plan[st]["A"][g]
                WCB, WC = blk(ic)
                WHB, WH = blk(ih)
                WBB, WB = blk(ib)
                js = slice(j * 128, (j + 1) * 128)
                hc = t4()
                cv = t4()
                xo = st * 16
                bx = None
                for half in range(2):
                    hs = slice(half * 512, (half + 1) * 512)
                    if half == 1:
                        bx = nb()
                    bc = nb()
                    fns = []
                    for kc in range(8):
                        fns.append(mm(ps[:, bc, :], WC[:, kc, js], xT[:, kc, hs], kc == 0, kc == 7))
                        if half == 1:
                            fns.append(mm(ps[:, bx, 0:2], WC[:, kc, js], xhT[:, xo + kc * 2:xo + kc * 2 + 2], kc == 0, kc == 7))
                    PEG(fns, reads=[WCB] + ([xhTB] if half == 1 else []) + xT_reads(half), writes=[bankB[bc]] + ([bankB[bx]] if half == 1 else []))
                    bh = nb()
                    fns = []
                    for kc in range(8):
                        fns.append(mm(ps[:, bh, :], WH[:, kc, js], xT[:, kc, hs], kc == 0, kc == 7))
                        if half == 1:
                            fns.append(mm(ps[:, bx, 2:4], WH[:, kc, js], xhT[:, xo + kc * 2:xo + kc * 2 + 2], kc == 0, kc == 7))
                    PEG(fns, reads=[WHB] + ([xhTB] if half == 1 else []) + xT_reads(half), writes=[bankB[bh]] + ([bankB[bx]] if half == 1 else []))
                    if j == 1 and half == 1:
                        release(ic)
                        release(ih)
                    o = 1 + half * 512
                    OP(act, lambda e, bc=bc, o=o, hc=hc: e.activation(out=hc.t[:, o:o + 512], in_=ps[:, bc, :], func=AF.Copy),
                       reads=[bankB[bc]], writes=[hc.B])
                    OP(dve, lambda e, bh=bh, o=o, hc=hc: e.tensor_tensor(out=hc.t[:, o:o + 512], in0=ps[:, bh, :], in1=hc.t[:, o:o + 512], op=ALU.mult),
                       reads=[bankB[bh], hc.B], writes=[hc.B])
                    if half == 1:
                        OP(act, lambda e, bx=bx, hc=hc: e.activation(out=hc.t[:, 0:1026:1025], in_=ps[:, bx, 0:2], func=AF.Copy),
                           reads=[bankB[bx]], writes=[hc.B])
                        OP(dve, lambda e, bx=bx, hc=hc: e.tensor_tensor(out=hc.t[:, 0:1026:1025], in0=ps[:, bx, 2:4], in1=hc.t[:, 0:1026:1025], op=ALU.mult),
                           reads=[bankB[bx], hc.B], writes=[hc.B])
                    side_work(st, ca, half)
                OP(act, lambda e, hc=hc, cv=cv, ca=ca: e.activation(out=cv.t[:, 0:1024], in_=hc.t[:, 1:1025], func=AF.Identity, scale=colv[:, CV_CW1 + ca:CV_CW1 + ca + 1]),
                   reads=[hc.B, colvB], writes=[cv.B])
                OP(dve, lambda e, hc=hc, cv=cv, ca=ca: e.scalar_tensor_tensor(out=cv.t[:, 0:1024], in0=hc.t[:, 0:1024], scalar=colv[:, CV_CW0 + ca:CV_CW0 + ca + 1], in1=cv.t[:, 0:1024], op0=ALU.mult, op1=ALU.add),
                   reads=[hc.B, cv.B, colvB], writes=[cv.B])
                OP(dve, lambda e, hc=hc, cv=cv, ca=ca: e.scalar_tensor_tensor(out=cv.t[:, 0:1024], in0=hc.t[:, 2:1026], scalar=colv[:, CV_CW2 + ca:CV_CW2 + ca + 1], in1=cv.t[:, 0:1024], op0=ALU.mult, op1=ALU.add),
                   reads=[hc.B, cv.B, colvB], writes=[cv.B])
                side_work(st, ca, 2)
                for half in range(2):
                    hs = slice(half * 512, (half + 1) * 512)
                    bbk = nb_slow()
                    PEG([mm(ps[:, bbk, :], WB[:, kc, js], xT[:, kc, hs], kc == 0, kc == 7) for kc in range(8)],
                        reads=[WBB] + xT_reads(half), writes=[bankB[bbk]])
                    OP(dve, lambda e, bbk=bbk, cv=cv, ca=ca, hs=hs: e.tensor_tensor(out=a_sb[:, ca, hs], in0=ps[:, bbk, :], in1=cv.t[:, hs], op=ALU.mult),
                       reads=[bankB[bbk], cv.B], writes=[aB[ca][half]])
                if j == 1:
                    release(ib)
                side_work(st, ca, 3)

        def v_unit(st, tt):
            if True:
                ts_ = slice(tt * 128, (tt + 1) * 128)
                b0 = nb2()
                fns = []
                for kc in range(8):
                    fns.append(mm(ps[:, b0, :], xT[:, kc, ts_], bigW[:, kc, 0:512], kc == 0, kc == 7))
                    fns.append(mm(ps[:, b0 + 1, :], xT[:, kc, ts_], bigW[:, kc, 512:1024], kc == 0, kc == 7))
                PEG(fns, reads=[bigWB] + xTB[tt], writes=[bankB[b0], bankB[b0 + 1]])
                gv = gvbuf()
                OP(act, lambda e, b0=b0, gv=gv: e.activation(out=gv.t[:, 0:1024].rearrange("p (a b) -> p a b", a=2), in_=ps[:, b0:b0 + 2, :], func=AF.Gelu_apprx_tanh),
                   reads=[bankB[b0], bankB[b0 + 1]], writes=[gv.B] + r2B[gv.i])
                stt_, sB, rc = ln_stats([gv.t[:, 0:512], gv.t[:, 512:1024]], [gv.B])

                def norm(gv=gv, tt=tt, stt_=stt_, rc=rc, sB=sB):
                    OP(act, lambda e: e.activation(out=vn_sb[:, tt, :], in_=gv.t[:, 0:1024], func=AF.Identity, scale=stt_[:, rc:rc + 1], bias=stt_[:, 21:22]),
                       reads=[gv.B, sB], writes=[vnB[tt]])
                defer(NORM_LAG, norm)

        def su_unit(st, h, half):
            if True:
                iu = plan[st]["U"][h // 2]
                UB, U = blk(iu)
                js = slice((h % 2) * 128, (h % 2 + 1) * 128)
                if True:
                    hs = slice(half * 512, (half + 1) * 512)
                    bs_ = nb()
                    PEG([mm(ps[:, bs_, c * 128:(c + 1) * 128], vn_sb[:, half * 4 + c, h * 128:(h + 1) * 128], wsTb[:, h, :], True, True) for c in range(4)],
                        reads=[wsTbB] + [vnB[half * 4 + c] for c in range(4)], writes=[bankB[bs_]])
                    bu = nb()
                    PEG([mm(ps[:, bu, :], U[:, kc, js], xT[:, kc, hs], kc == 0, kc == 7) for kc in range(8)],
                        reads=[UB] + xT_reads(half), writes=[bankB[bu]])
                    gu = t2()
                    m = t2()
                    OP(act, lambda e, bu=bu, gu=gu: e.activation(out=gu.t[:, :], in_=ps[:, bu, :], func=AF.Gelu_apprx_tanh),
                       reads=[bankB[bu]], writes=[gu.B])
                    OP(dve, lambda e, bs_=bs_, m=m, h=h: e.scalar_tensor_tensor(out=m.t[:, :].rearrange("p (c i) -> p c i", c=4), in0=ps[:, bs_, :].rearrange("p (c i) -> p c i", c=4),
                                                                                scalar=colv[:, CV_VG + h:CV_VG + h + 1], in1=Cc[:, h:h + 1, :].broadcast_to([128, 4, 128]),
                                                                                op0=ALU.mult, op1=ALU.add),
                       reads=[bankB[bs_], CcB, colvB], writes=[m.B])
                    OP(dve, lambda e, m=m, gu=gu, h=h, hs=hs: e.tensor_tensor(out=bb_sb[:, h, hs], in0=m.t[:, :], in1=gu.t[:, :], op=ALU.mult),
                       reads=[m.B, gu.B], writes=[bbB[h][half]])
                if h % 2 == 1 and half == 1:
                    release(iu)

        def stage_YZ(st):
            nbanks[0] = 8
            for oc in range(8):
                g, j = oc // 2, oc % 2
                ipa0, ipa1, ipb, iga, igb = plan[st]["YZ"][g]
                PA0B, PA0 = blk(ipa0)
                PA1B, PA1 = blk(ipa1)
                PBB, PB = blk(ipb)
                GAB, GA = blk(iga)
                GBB, GB = blk(igb)
                js = slice(j * 128, (j + 1) * 128)
                for half in range(2):
                    hs = slice(half * 512, (half + 1) * 512)
                    bya = nb()
                    PEG([mm(ps[:, bya, :], (PA0[:, ca, js] if ca < 8 else PA1[:, ca - 8, js]), a_sb[:, ca, hs], ca == 0, ca == 11) for ca in range(12)],
                        reads=[PA0B, PA1B] + [aB[ca][half] for ca in range(12)], writes=[bankB[bya]])
                    if j == 1 and half == 1:
                        release(ipa0)
                        release(ipa1)
                    bga = nb()
                    PEG([mm(ps[:, bga, :], GA[:, kc, js], xT[:, kc, hs], kc == 0, kc == 7) for kc in range(8)],
                        reads=[GAB] + xT_reads(half), writes=[bankB[bga]])
                    if j == 1 and half == 1:
                        release(iga)
                    byb = nb()
                    PEG([mm(ps[:, byb, :], PB[:, h, js], bb_sb[:, h, hs], h == 0, h == 7) for h in range(8)],
                        reads=[PBB] + [bbB[h][half] for h in range(8)], writes=[bankB[byb]])
                    if j == 1 and half == 1:
                        release(ipb)
                    bgb = nb()
                    PEG([mm(ps[:, bgb, :], GB[:, kc, js], xT[:, kc, hs], kc == 0, kc == 7) for kc in range(8)],
                        reads=[GBB] + xT_reads(half), writes=[bankB[bgb]])
                    if j == 1 and half == 1:
                        release(igb)
                    sga = t2()
                    sgb = t2()
                    OP(act, lambda e, bga=bga, sga=sga, oc=oc: e.activation(out=sga.t[:, :], in_=ps[:, bga, :], func=AF.Sigmoid, bias=colv[:, CV_BGA + oc:CV_BGA + oc + 1]),
                       reads=[bankB[bga], colvB], writes=[sga.B])
                    OP(act, lambda e, bgb=bgb, sgb=sgb, oc=oc: e.activation(out=sgb.t[:, :], in_=ps[:, bgb, :], func=AF.Sigmoid, bias=colv[:, CV_BGB + oc:CV_BGB + oc + 1]),
                       reads=[bankB[bgb], colvB], writes=[sgb.B])
                    OP(dve, lambda e, bya=bya, sga=sga: e.tensor_tensor(out=sga.t[:, :], in0=ps[:, bya, :], in1=sga.t[:, :], op=ALU.mult),
                       reads=[bankB[bya], sga.B], writes=[sga.B])
                    OP(dve, lambda e, byb=byb, sgb=sgb: e.tensor_tensor(out=sgb.t[:, :], in0=ps[:, byb, :], in1=sgb.t[:, :], op=ALU.mult),
                       reads=[bankB[byb], sgb.B], writes=[sgb.B])
                    OP(dve, lambda e, sga=sga, sgb=sgb, oc=oc, hs=hs: e.tensor_tensor(out=z_sb[:, oc, hs], in0=sga.t[:, :], in1=sgb.t[:, :], op=ALU.add),
                       reads=[sga.B, sgb.B], writes=[zB[oc][half]])

        def stage_O(st):
            info = {}

            def head(tt):
                ts_ = slice(tt * 128, (tt + 1) * 128)
                half = tt // 4
                xr = t4()
                r0 = st * T + tt * 128
                DMA(sp, xr.t[:, 0:1024], x_d[r0:r0 + 128, :], xr.ld, writes=[xr.B])
                b0 = nb2()
                fns = []
                for kc in range(8):
                    fns.append(mm(ps[:, b0, :], z_sb[:, kc, ts_], bigW[:, kc, 0:512], kc == 0, kc == 7))
                    fns.append(mm(ps[:, b0 + 1, :], z_sb[:, kc, ts_], bigW[:, kc, 512:1024], kc == 0, kc == 7))
                PEG(fns, reads=[bigWB] + [zB[kc][half] for kc in range(8)], writes=[bankB[b0], bankB[b0 + 1]])
                OP(dve, lambda e: e.scalar_tensor_tensor(out=xr.t[:, 0:1024].rearrange("p (a b) -> p a b", a=2), in0=xr.t[:, 0:1024].rearrange("p (a b) -> p a b", a=2),
                                                         scalar=ALPHA, in1=ps[:, b0:b0 + 2, :], op0=ALU.mult, op1=ALU.add),
                   reads=[bankB[b0], bankB[b0 + 1], xr.B], writes=[xr.B])
                stt_, sB, rc = ln_stats([xr.t[:, 0:512], xr.t[:, 512:1024]], [xr.B])
                info[tt] = (xr, stt_, sB, rc)

            def norm(tt):
                xr, stt_, sB, rc = info[tt]
                OP(act, lambda e: e.activation(out=xr.t[:, 0:1024], in_=xr.t[:, 0:1024], func=AF.Identity, scale=stt_[:, rc:rc + 1], bias=stt_[:, 21:22]),
                   reads=[xr.B, sB], writes=[xr.B])

            def tail(tt):
                xr = info[tt][0]
                ts_ = slice(tt * 128, (tt + 1) * 128)
                b0 = nb2()
                PEG([tr(ps[:, b0 + kc // 4, (kc % 4) * 128:(kc % 4 + 1) * 128], xr.t[:, kc * 128:(kc + 1) * 128]) for kc in range(8)],
                    reads=[xr.B, memB], writes=[bankB[b0], bankB[b0 + 1]], do_tick=False)
                for kc in range(4):
                    src = ps[:, b0, kc * 128:(kc + 1) * 128]
                    OP(act, lambda e, src=src, kc=kc: e.activation(out=x1Tf[:, kc, ts_], in_=src, func=AF.Identity, scale=colv[:, CV_L1G + kc:CV_L1G + kc + 1], bias=colv[:, CV_L1B + kc:CV_L1B + kc + 1]),
                       reads=[bankB[b0], colvB], writes=[x1fB[tt][0]])
                g1b = colv[:, CV_L1G + 4:CV_L1G + 8].unsqueeze(2).broadcast_to([128, 4, 128])
                b1b = colv[:, CV_L1B + 4:CV_L1B + 8].unsqueeze(2).broadcast_to([128, 4, 128])
                OP(dve, lambda e: e.tensor_tensor(out=x1Tf[:, 4:8, ts_], in0=ps[:, b0 + 1, :].rearrange("p (k n) -> p k n", k=4), in1=g1b, op=ALU.mult),
                   reads=[bankB[b0 + 1], colvB], writes=[x1fB[tt][1]])
                OP(dve, lambda e: e.tensor_tensor(out=x1Tf[:, 4:8, ts_], in0=x1Tf[:, 4:8, ts_], in1=b1b, op=ALU.add),
                   reads=[x1fB[tt][1], colvB], writes=[x1fB[tt][1]])

            def cast(tt):
                ts_ = slice(tt * 128, (tt + 1) * 128)
                OP(act, lambda e: e.activation(out=x1Tb[:, :, ts_], in_=x1Tf[:, :, ts_], func=AF.Copy), reads=x1fB[tt], writes=[x1bB[tt]])

            def step(k):
                if 0 <= k - 3 < 8:
                    cast(k - 3)
                if k < 8:
                    head(k)
                if 0 <= k - 1 < 8:
                    norm(k - 1)
                if 0 <= k - 2 < 8:
                    tail(k - 2)

            for k in range(8):
                step(k)
            if st + 1 < ST:
                DMA(pool, bigW[:], winr[:, :, OFF_VB:OFF_VB + 1024], bigW_sem, writes=[bigWB])
            return [lambda k=k: step(k) for k in range(8, 11)]

        def f1_group(st, fc, half):
            i1 = plan[st]["F1"][fc // 2]
            WB_, W = blk(i1)
            js = slice((fc % 2) * 128, (fc % 2 + 1) * 128)
            hs = slice(half * 512, (half + 1) * 512)
            b = nb()
            PEG([mm(ps[:, b, :], W[:, kc, js], x1Tb[:, kc, hs], kc == 0, kc == 7) for kc in range(8)],
                reads=[WB_] + [x1bB[tt] for tt in range(half * 4, half * 4 + 4)], writes=[bankB[b]])
            r = t2()
            OP(act, lambda e, b=b, r=r: e.activation(out=r.t[:, :], in_=ps[:, b, :], func=AF.Relu), reads=[bankB[b]], writes=[r.B])
            OP(dve, lambda e, r=r, fc=fc, hs=hs: e.tensor_tensor(out=hid[:, fc, hs], in0=r.t[:, :], in1=r.t[:, :], op=ALU.mult),
               reads=[r.B], writes=[hidB[fc][half]] + ([NX[0].B, NX[1].B] if fc >= 28 else []))

        def bigq_views():
            bigv = bigW[:].rearrange("p k n -> p (k n)")
            return [bigv[:, q * 2048:(q + 1) * 2048].rearrange("p (k n) -> p k n", k=8) for q in range(4)]

        def regr_views():
            rv = regR[:].rearrange("p k n -> p (k n)")
            return [rv[:, q * 2048:(q + 1) * 2048].rearrange("p (k n) -> p k n", k=8) for q in range(4)]

        preB = {"bigW": Buf(accum=True), "regR": Buf(accum=True)}

        def preload(views, g, sem, war_bufs, key):
            for q in range(4):
                DMA(pool, views[q], wf2r[:, q * 8:(q + 1) * 8, g * 256:(g + 1) * 256], sem,
                    writes=[preB[key]] + (war_bufs if q == 0 else []))

        def stage_F1(st, drain):
            G0 = 6
            if st == ST - 1:
                preload(bigq_views(), 0, bigW_sem, [bigWB], "bigW")
            n = 0
            for g in range(G0):
                for fc in (2 * g, 2 * g + 1):
                    f1_group(st, fc, 0)
                    n += 1
                    if n % 2 == 0 and drain:
                        drain.pop(0)()
            while drain:
                drain.pop(0)()
            flush_deferred()
            for g in range(G0):
                for fc in (2 * g, 2 * g + 1):
                    f1_group(st, fc, 1)
                release(plan[st]["F1"][g])
            for g in range(G0, 16):
                for fc in (2 * g, 2 * g + 1):
                    f1_group(st, fc, 0)
                    f1_group(st, fc, 1)
                release(plan[st]["F1"][g])

        out_toks = []
        pending_tails = []

        def ln2_tail(st, tt):
            half = tt // 4
            ts_ = slice(tt * 128, (tt + 1) * 128)
            b0 = nb2()
            PEG([tr(ps[:, b0 + oc // 4, (oc % 4) * 128:(oc % 4 + 1) * 128], x1Tf[:, oc, ts_]) for oc in range(8)],
                reads=[memB] + [r2B[oc][tt // 2] for oc in range(8)], writes=[bankB[b0], bankB[b0 + 1]], do_tick=False)
            if st < ST - 1:
                k = nxc[0] % 3
                nxc[0] += 1
                n2 = NX[k] if k < 2 else t4()
            else:
                n2 = t4()
            OP(act, lambda e: e.activation(out=n2.t[:, 0:1024].rearrange("p (a b) -> p a b", a=2), in_=ps[:, b0:b0 + 2, :], func=AF.Copy),
               reads=[bankB[b0], bankB[b0 + 1]], writes=[n2.B])
            stt_, sB, rc = ln_stats([n2.t[:, 0:512], n2.t[:, 512:1024]], [n2.B])

            def phase2():
                OP(act, lambda e: e.activation(out=n2.t[:, 0:1024], in_=n2.t[:, 0:1024], func=AF.Identity, scale=stt_[:, rc:rc + 1], bias=stt_[:, 21:22]),
                   reads=[n2.B, sB], writes=[n2.B])
                OP(dve, lambda e: e.tensor_tensor(out=n2.t[:, 0:1024], in0=n2.t[:, 0:1024], in1=g2rep[:, :], op=ALU.mult), reads=[n2.B, g2B], writes=[n2.B])
                OP(dve, lambda e: e.tensor_tensor(out=n2.t[:, 0:1024], in0=n2.t[:, 0:1024], in1=b2rep[:, :], op=ALU.add), reads=[n2.B, b2B], writes=[n2.B])
                r0 = st * T + tt * 128
                tok = DMA(sp, y_d[r0:r0 + 128, :], n2.t[:, 0:1024], n2.st, reads=[n2.B])
                out_toks.append(tok)
            defer(NORM_LAG, phase2)

        def queue_tails(st, tts=tuple(range(8))):
            for tt in tts:
                pending_tails.append(lambda st=st, tt=tt: ln2_tail(st, tt))

        def f2_group(st, Ws, oc, j, half, quarter=None):
            js = slice(j * 128, (j + 1) * 128)
            if quarter is None:
                hs = slice(half * 512, (half + 1) * 512)
                qs = [2 * half, 2 * half + 1]
                tts = range(half * 4, half * 4 + 4)
                n = 512
            else:
                hs = slice(half * 512 + quarter * 256, half * 512 + (quarter + 1) * 256)
                qs = [2 * half + quarter]
                tts = range(half * 4 + quarter * 2, half * 4 + quarter * 2 + 2)
                n = 256
            b = nb()
            PEG([mm(ps[:, b, 0:n], Ws[fc // 8][1][:, fc % 8, js], hid[:, fc, hs], fc == 0, fc == 31) for fc in range(32)],
                reads=[w[0] for w in Ws] + [hidB[fc][half] for fc in range(32)], writes=[bankB[b]])
            OP(dve, lambda e, b=b, oc=oc, hs=hs, n=n: e.scalar_tensor_tensor(out=x1Tf[:, oc, hs], in0=x1Tf[:, oc, hs], scalar=ALPHA, in1=ps[:, b, 0:n], op0=ALU.mult, op1=ALU.add),
               reads=[bankB[b]] + [x1fB[tt][oc // 4] for tt in tts], writes=[r2B[oc][q] for q in qs])

        def stage_F2(st):
            if st < ST - 1:
                load_xT(st + 1, 0)
                load_xT(st + 1, 1)
                for g in range(4):
                    ids = plan[st]["F2"][g]
                    Ws = [blk(i) for i in ids]
                    for j in range(2):
                        for half in range(2):
                            f2_group(st, Ws, g * 2 + j, j, half)
                    for i in ids:
                        release(i)
                queue_tails(st)
            else:
                passes = [("F2", 0, None, (0, 1, 2, 3)), ("F2b", 1, None, (4, 5, 6, 7))]
                bigq = bigq_views()
                regq = regr_views()
                preload(regq, 1, xT_sem[0], [x1bB[tt] for tt in range(8)] + [xTB[tt][k] for tt in range(8) for k in range(2)], "regR")
                for key, half, quarter, tts in passes:
                    for g in range(4):
                        ids = plan[st][key][g]
                        if ids == "bigW":
                            Ws = [(preB["bigW"], bigq[q]) for q in range(4)]
                            ids = None
                        elif ids == "regR":
                            Ws = [(preB["regR"], regq[q]) for q in range(4)]
                            ids = None
                        else:
                            Ws = [blk(i) for i in ids]
                        for j in range(2):
                            f2_group(st, Ws, g * 2 + j, j, half, quarter)
                            if pending_tails and half == 1 and j == 0:
                                pending_tails.pop(0)()
                        for i in (ids or []):
                            release(i)
                    queue_tails(st, tts)

        setup_Cc()
        for st in range(ST):
            stage_A(st)
            stage_YZ(st)
            drain = stage_O(st)
            stage_F1(st, drain)
            stage_F2(st)
        flush_deferred()
        while pending_tails:
            pending_tails.pop(0)()
        flush_deferred()
        sp.wait(out_toks)

        with nc.Block() as block:
            @block.tensor
            def _(e):
                for it in pe.items:
                    it(e)

            @block.scalar
            def _(e):
                for it in act.items:
                    it(e)

            @block.vector
            def _(e):
                for it in dve.items:
                    it(e)

            @block.gpsimd
            def _(e):
                for it in pool.items:
                    it(e)

            @block.sync
            def _(e):
                for it in sp.items:
                    it(e)
    return nc


_NC_CACHE = {}


def _host_layout(x, conv_w, b_gate, v_norm_g, v_norm_b, w_s, b_s, ln1_g, ln1_b):
    xs = np.ascontiguousarray(x[0])
    cvs = np.concatenate([
        conv_w[0].reshape(3, 12, 128).transpose(2, 0, 1).reshape(128, 36),
        b_gate[0].reshape(16, 128).T,
        v_norm_g[0].reshape(8, 128).T,
        v_norm_b[0].reshape(8, 128).T,
        ln1_g[0].reshape(8, 128).T,
        ln1_b[0].reshape(8, 128).T,
    ], axis=1)
    cvs = np.ascontiguousarray(cvs, dtype=np.float32)
    assert cvs.shape == (128, NCV)
    wsT = np.ascontiguousarray(w_s[0].transpose(2, 0, 1).reshape(128, 1024))
    bsrow = np.ascontiguousarray(b_s[0].reshape(1, 1024))
    xpad = np.concatenate([np.zeros((1, D), np.float32), xs, np.zeros((1, D), np.float32)], axis=0)
    per_core = []
    for c in range(NCORES):
        xh = np.zeros((ST, 2, D), np.float32)
        for st in range(ST):
            g0 = c * TOK + st * T
            xh[st, 0] = xpad[g0]
            xh[st, 1] = xpad[g0 + T + 1]
        xhT = np.ascontiguousarray(xh.reshape(ST, 2, 8, 128).transpose(3, 0, 2, 1).reshape(128, ST * 16))
        xc = np.ascontiguousarray(xs[c * TOK:(c + 1) * TOK])
        per_core.append((xc, xhT, np.ascontiguousarray(xc.T)))
    return cvs, wsT, bsrow, per_core


def kernel(x, w_in, b_gate, conv_w, v_norm_g, v_norm_b, w_s, b_s, w_pa, w_pb, w_o,
           ln1_g, ln1_b, w_ff1, w_ff2, ln2_g, ln2_b):
    f = lambda a: np.ascontiguousarray(np.asarray(a, dtype=np.float32))
    x, w_in, b_gate, conv_w, v_norm_g, v_norm_b, w_s, b_s = map(f, (x, w_in, b_gate, conv_w, v_norm_g, v_norm_b, w_s, b_s))
    w_pa, w_pb, w_o, ln1_g, ln1_b, w_ff1, w_ff2, ln2_g, ln2_b = map(f, (w_pa, w_pb, w_o, ln1_g, ln1_b, w_ff1, w_ff2, ln2_g, ln2_b))
    cvs, wsT, bsrow, per_core = _host_layout(x, conv_w, b_gate, v_norm_g, v_norm_b, w_s, b_s, ln1_g, ln1_b)
    if "nc" not in _NC_CACHE:
        _NC_CACHE["nc"] = build_program()
    nc = _NC_CACHE["nc"]
    shared = {
        "w_in": w_in[0], "w_pa": w_pa[0], "w_pb": w_pb[0], "w_o": w_o[0], "w_ff1": w_ff1[0], "w_ff2": w_ff2[0],
        "colvecs": cvs, "wsT": wsT, "bsrow": bsrow, "ln2g": ln2_g.reshape(1, 1024), "ln2b": ln2_b.reshape(1, 1024),
    }
    in_maps = []
    for c in range(NCORES):
        m = dict(shared)
        m["x"] = per_core[c][0]
        m["xhT"] = per_core[c][1]
        m["xT"] = per_core[c][2]
        in_maps.append(m)
    res = run_bass_kernel_spmd(nc, in_maps, core_ids=list(range(NCORES)))
    out = np.concatenate([r["y"] for r in res.results], axis=0)
    return out.reshape(1, SEQ, D).astype(np.float32)
```
